# Optimizing a Trainium2 kernel written in Bass

```python
import math
import jax, jax.numpy as jnp
from jax import lax
import numpy as np

D_MODEL = 1024
BATCH = 32
SEQ = 256
DEPTH = 4
DEC_BATCH = 4
DEC_SEQ = 4096
PAST_LEN = 512

GRID_W = 64
N_EVEN = (DEPTH + 1) // 2
N_ODD = DEPTH // 2
HEAD_DIM = 64
ATTN_SCALE = HEAD_DIM ** -0.5
ROPE_BASE = 10000.0
RMS_EPS = 1e-6
N_MOD = 6
A_HEADS = 8
A_KV_HEADS = 2
A_GROUP = A_HEADS // A_KV_HEADS
A_WINDOW = 128
A_BLOCK = 128
A_WIDTH = A_HEADS * HEAD_DIM
B_HEADS = 4
B_DK = 64
B_DV = 128
B_GATE_RANK = 16
B_GATE_TAU = 16.0
B_CHUNK = 64
B_WIDTH = B_HEADS * B_DV
C_HEADS = 8
C_DV = 2 * HEAD_DIM
C_BLOCK = 128
C_WIDTH = C_HEADS * C_DV
EVEN_SPLITS = (A_HEADS * HEAD_DIM, A_KV_HEADS * HEAD_DIM, A_KV_HEADS * HEAD_DIM, B_HEADS * B_DK, B_HEADS * B_DK, B_WIDTH, B_WIDTH, 2 * B_GATE_RANK)
EVEN_IN = sum(EVEN_SPLITS)
EVEN_OUT = A_WIDTH + B_WIDTH
ODD_IN = 3 * C_WIDTH
P_HEADS = 8
P_NKEYS = 128
P_EXPERTS = P_NKEYS * P_NKEYS
P_DKEY = 128
P_TOPK = 16
P_BLOCK = 128

kernel_name = 'hybrid_diffusion_prefix_trunk_step'

F32 = jnp.float32


def rmsnorm(x, g):
    xf = x.astype(F32)
    y = xf * lax.rsqrt(jnp.mean(xf * xf, axis=-1, keepdims=True) + RMS_EPS)
    return (y * g.astype(F32)).astype(x.dtype)


def modulation(cvec, w, b):
    m = (jax.nn.silu(cvec) @ w + b)[:, None, :]
    return jnp.split(m, N_MOD, axis=-1)


def rope_2d(T):
    rows = T // GRID_W
    t = jnp.arange(rows * GRID_W)
    row = (t // GRID_W).astype(F32)
    col = (t % GRID_W).astype(F32)
    nf = HEAD_DIM // 4
    freqs = ROPE_BASE ** (-jnp.arange(nf, dtype=F32) / nf)
    ang = jnp.stack([row[:, None] * freqs, col[:, None] * freqs], axis=1)
    return jnp.cos(ang), jnp.sin(ang)


def apply_rope_2d(x, cos, sin):
    extra = x.ndim - 3
    shp = (cos.shape[0],) + (1,) * extra + cos.shape[1:]
    cs = cos.reshape(shp)
    sn = sin.reshape(shp)
    xr = x.astype(F32).reshape(x.shape[:-1] + (2, 2, HEAD_DIM // 4))
    x1 = xr[..., 0, :]
    x2 = xr[..., 1, :]
    out = jnp.stack([x1 * cs - x2 * sn, x2 * cs + x1 * sn], axis=-2)
    return out.reshape(x.shape).astype(x.dtype)


def attn_sink_dense(q, k, v, sink):
    B, Tq = q.shape[:2]
    s = jnp.einsum('bqkgd,bckd->bkgqc', q, k).astype(F32) * ATTN_SCALE
    sk = jnp.broadcast_to(sink.astype(F32)[None, :, :, None, None], s.shape[:-1] + (1,))
    p = jax.nn.softmax(jnp.concatenate([s, sk], axis=-1), axis=-1)[..., :-1]
    out = jnp.einsum('bkgqc,bckd->bqkgd', p.astype(v.dtype), v)
    return out.reshape(B, Tq, A_WIDTH)


def attn_window_ctx(q, k, v, kc, vc, sink):
    B, T = q.shape[:2]
    W = A_BLOCK
    NB = T // W
    Lc = kc.shape[1]
    qb = q.reshape(B, NB, W, A_KV_HEADS, A_GROUP, HEAD_DIM)

    def band(x):
        xp = jnp.pad(x, ((0, 0), (W, W), (0, 0), (0, 0))).reshape(B, NB + 2, W, A_KV_HEADS, HEAD_DIM)
        return jnp.concatenate([xp[:, :-2], xp[:, 1:-1], xp[:, 2:]], axis=2)

    kw = band(k)
    vw = band(v)
    s_w = jnp.einsum('bnikgd,bnjkd->bnkgij', qb, kw).astype(F32) * ATTN_SCALE
    blk = jnp.arange(NB)[:, None, None] * W
    qpos = blk + jnp.arange(W)[None, :, None]
    kpos = blk - W + jnp.arange(3 * W)[None, None, :]
    valid = (jnp.abs(qpos - kpos) <= A_WINDOW) & (kpos >= 0) & (kpos < T)
    s_w = jnp.where(valid[None, :, None, None], s_w, -jnp.inf)
    s_c = jnp.einsum('bnikgd,bckd->bnkgic', qb, kc).astype(F32) * ATTN_SCALE
    sk = jnp.broadcast_to(sink.astype(F32)[None, None, :, :, None, None], s_c.shape[:-1] + (1,))
    p = jax.nn.softmax(jnp.concatenate([s_w, s_c, sk], axis=-1), axis=-1)
    pw = p[..., :3 * W].astype(v.dtype)
    pc = p[..., 3 * W:3 * W + Lc].astype(v.dtype)
    out = jnp.einsum('bnkgij,bnjkd->bnikgd', pw, vw) + jnp.einsum('bnkgic,bckd->bnikgd', pc, vc)
    return out.reshape(B, T, A_WIDTH)


def gla_chunked(q, k, v, lg, s0):
    B, H, T, dk = q.shape
    dv = v.shape[-1]
    L = B_CHUNK
    N = T // L
    q = q.reshape(B, H, N, L, dk)
    k = k.reshape(B, H, N, L, dk)
    v = v.reshape(B, H, N, L, dv)
    b = jnp.cumsum(lg.reshape(B, H, N, L, dk), axis=3)
    b_last = b[:, :, :, -1:, :]
    qd = q * jnp.exp(b)
    kd = k * jnp.exp(-b)
    kt = k * jnp.exp(b_last - b)
    causal = jnp.tril(jnp.ones((L, L), dtype=bool))
    a = jnp.where(causal, jnp.einsum('bhnid,bhnjd->bhnij', qd, kd), 0.0)
    o_intra = jnp.einsum('bhnij,bhnjv->bhniv', a, v)
    kv = jnp.einsum('bhnjd,bhnjv->nbhdv', kt, v)
    decay = jnp.moveaxis(jnp.exp(b_last[:, :, :, 0, :]), 2, 0)

    def step(s, inp):
        dec, kvn = inp
        return dec[..., None] * s + kvn, s

    s_fin, s_starts = lax.scan(step, s0, (decay, kv))
    o_inter = jnp.einsum('bhnid,nbhdv->bhniv', qd, s_starts)
    return (o_intra + o_inter).reshape(B, H, T, dv), s_fin


def gla_bidir(q, k, v, lg_f, lg_b, s0_f, s0_b):
    def tr(x):
        return jnp.swapaxes(x, 1, 2).astype(F32)

    def fl(x):
        return jnp.flip(x, axis=2)

    qh = tr(q) * (B_DK ** -0.5)
    kh = tr(k)
    vh = tr(v)
    of, sf = gla_chunked(qh, kh, vh, tr(lg_f), s0_f.astype(F32))
    ob, sb = gla_chunked(fl(qh), fl(kh), fl(vh), fl(tr(lg_b)), s0_b.astype(F32))
    o = of + fl(ob)
    return jnp.swapaxes(o, 1, 2), sf, sb


def even_project(h, w_in, a_qn, a_kn, gw_f, gb_f, gw_b, gb_b):
    B, T, _ = h.shape
    offs = [int(o) for o in np.cumsum(EVEN_SPLITS)[:-1]]
    aq, ak, av, bq, bk, bv, br, bg = jnp.split(h @ w_in, offs, axis=-1)
    aq = rmsnorm(aq.reshape(B, T, A_KV_HEADS, A_GROUP, HEAD_DIM), a_qn)
    ak = rmsnorm(ak.reshape(B, T, A_KV_HEADS, HEAD_DIM), a_kn)
    av = av.reshape(B, T, A_KV_HEADS, HEAD_DIM)
    bq = bq.reshape(B, T, B_HEADS, B_DK)
    bk = bk.reshape(B, T, B_HEADS, B_DK)
    bv = bv.reshape(B, T, B_HEADS, B_DV)
    bgf, bgb = jnp.split(bg, 2, axis=-1)
    lg_f = (jax.nn.log_sigmoid((bgf @ gw_f + gb_f).astype(F32)) / B_GATE_TAU).reshape(B, T, B_HEADS, B_DK)
    lg_b = (jax.nn.log_sigmoid((bgb @ gw_b + gb_b).astype(F32)) / B_GATE_TAU).reshape(B, T, B_HEADS, B_DK)
    return aq, ak, av, bq, bk, bv, br, lg_f, lg_b


def even_output(oa, ob, br, b_on, w_out):
    B, T, _ = oa.shape
    ob = rmsnorm(ob.astype(br.dtype), b_on).reshape(B, T, B_WIDTH) * jax.nn.silu(br)
    return jnp.concatenate([oa, ob.astype(oa.dtype)], axis=-1) @ w_out


def odd_project(h, w_in, q_norm, k_norm):
    B, T, _ = h.shape
    q, k, v = jnp.split(h @ w_in, 3, axis=-1)
    q = rmsnorm(q.reshape(B, T, C_HEADS, 2, HEAD_DIM), q_norm)
    k = rmsnorm(k.reshape(B, T, C_HEADS, 2, HEAD_DIM), k_norm)
    return q, k, v.reshape(B, T, C_HEADS, C_DV)


def diff_lambda(lq1, lk1, lq2, lk2, lam_init):
    return (jnp.exp(jnp.sum(lq1.astype(F32) * lk1.astype(F32))) - jnp.exp(jnp.sum(lq2.astype(F32) * lk2.astype(F32))) + lam_init)


def diff_attention(q, k, v, lam):
    B, Tq = q.shape[:2]
    NB = Tq // C_BLOCK
    qb = jnp.moveaxis(q.reshape(B, NB, C_BLOCK, C_HEADS, 2, HEAD_DIM), 1, 0)

    def block(qi):
        s = jnp.einsum('bihmd,bjhmd->bhmij', qi, k).astype(F32) * ATTN_SCALE
        p = jax.nn.softmax(s, axis=-1)
        pd = p[:, :, 0] - lam * p[:, :, 1]
        return jnp.einsum('bhij,bjhv->bihv', pd.astype(v.dtype), v)

    o = lax.map(block, qb)
    return jnp.moveaxis(o, 0, 1).reshape(B, Tq, C_HEADS, C_DV)


def odd_output(o, c_on, lam_init, w_out):
    B, T = o.shape[:2]
    o = rmsnorm(o, c_on) * (1.0 - lam_init)
    return o.reshape(B, T, C_WIDTH) @ w_out


def lambda_init(layer):
    return 0.8 - 0.6 * math.exp(-0.3 * layer)


def peer(h, wq, sub_keys, u, v):
    B, T, D = h.shape
    xs = h.reshape(-1, P_BLOCK, D)

    def block(xc):
        q = (xc @ wq).reshape(P_BLOCK, P_HEADS, 2, P_DKEY // 2)
        s = jnp.einsum('chpd,hpkd->chpk', q, sub_keys).astype(F32)
        sv, si = lax.top_k(s, P_TOPK)
        cand = (sv[:, :, 0, :, None] + sv[:, :, 1, None, :]).reshape(P_BLOCK, P_HEADS, P_TOPK * P_TOPK)
        cidx = (si[:, :, 0, :, None] * P_NKEYS + si[:, :, 1, None, :]).reshape(P_BLOCK, P_HEADS, P_TOPK * P_TOPK)
        fs, fi = lax.top_k(cand, P_TOPK)
        eidx = jnp.take_along_axis(cidx, fi, axis=-1)
        g = jax.nn.softmax(fs, axis=-1)
        act = jax.nn.gelu(jnp.einsum('cd,chkd->chk', xc, u[eidx]).astype(F32), approximate=False)
        return jnp.einsum('chk,chkd->cd', (g * act).astype(v.dtype), v[eidx])

    return lax.map(block, xs).reshape(B, T, D)


def setup_inputs(seed: int = 0) -> dict:
    key = jax.random.key(seed)
    ks = iter(jax.random.split(key, 64))
    D = D_MODEL

    def nrm(shape, scale):
        return jax.random.normal(next(ks), shape, jnp.float32) * scale

    def gain(shape):
        return 1.0 + nrm(shape, 0.1)

    return {
        'x_prompt': nrm((BATCH, SEQ, D), 1.0),
        'x_sample': nrm((DEC_BATCH, DEC_SEQ, D), 1.0),
        'cache_a_k': nrm((DEC_BATCH, N_EVEN, PAST_LEN, A_KV_HEADS, HEAD_DIM), 1.0),
        'cache_a_v': nrm((DEC_BATCH, N_EVEN, PAST_LEN, A_KV_HEADS, HEAD_DIM), 1.0),
        'state_b_fwd': nrm((DEC_BATCH, N_EVEN, B_HEADS, B_DK, B_DV), 0.5),
        'state_b_bwd': nrm((DEC_BATCH, N_EVEN, B_HEADS, B_DK, B_DV), 0.5),
        'cache_c_k': nrm((DEC_BATCH, N_ODD, PAST_LEN, C_HEADS, 2, HEAD_DIM), 1.0),
        'cache_c_v': nrm((DEC_BATCH, N_ODD, PAST_LEN, C_HEADS, C_DV), 1.0),
        'c': nrm((DEC_BATCH, D), 1.0),
        'c_ctx': nrm((D,), 1.0),
        'ada_w': nrm((DEPTH, D, N_MOD * D), 0.5 * D ** -0.5),
        'ada_b': nrm((DEPTH, N_MOD * D), 0.02),
        'norm_mix_g': gain((DEPTH, D)),
        'norm_ffn_g': gain((DEPTH, D)),
        'e_w_in': nrm((N_EVEN, D, EVEN_IN), D ** -0.5),
        'e_w_out': nrm((N_EVEN, EVEN_OUT, D), EVEN_OUT ** -0.5),
        'a_q_norm': gain((N_EVEN, HEAD_DIM)),
        'a_k_norm': gain((N_EVEN, HEAD_DIM)),
        'a_sink': nrm((N_EVEN, A_KV_HEADS, A_GROUP), 0.5),
        'b_gate_w_f': nrm((N_EVEN, B_GATE_RANK, B_HEADS * B_DK), B_GATE_RANK ** -0.5),
        'b_gate_b_f': nrm((N_EVEN, B_HEADS * B_DK), 0.1),
        'b_gate_w_b': nrm((N_EVEN, B_GATE_RANK, B_HEADS * B_DK), B_GATE_RANK ** -0.5),
        'b_gate_b_b': nrm((N_EVEN, B_HEADS * B_DK), 0.1),
        'b_out_norm': gain((N_EVEN, B_DV)),
        'o_w_in': nrm((N_ODD, D, ODD_IN), D ** -0.5),
        'o_w_out': nrm((N_ODD, C_WIDTH, D), C_WIDTH ** -0.5),
        'c_q_norm': gain((N_ODD, HEAD_DIM)),
        'c_k_norm': gain((N_ODD, HEAD_DIM)),
        'c_lambda_q1': nrm((N_ODD, HEAD_DIM), 0.1),
        'c_lambda_k1': nrm((N_ODD, HEAD_DIM), 0.1),
        'c_lambda_q2': nrm((N_ODD, HEAD_DIM), 0.1),
        'c_lambda_k2': nrm((N_ODD, HEAD_DIM), 0.1),
        'c_out_norm': gain((N_ODD, C_DV)),
        'p_w_q': nrm((DEPTH, D, P_HEADS * P_DKEY), D ** -0.5),
        'p_sub_keys': nrm((DEPTH, P_HEADS, 2, P_NKEYS, P_DKEY // 2), (P_DKEY // 2) ** -0.5),
        'p_u': nrm((DEPTH, P_EXPERTS, D), D ** -0.5),
        'p_v': nrm((DEPTH, P_EXPERTS, D), 0.25),
    }


def reference(x_prompt, x_sample, cache_a_k, cache_a_v, state_b_fwd, state_b_bwd, cache_c_k, cache_c_v, c, c_ctx, ada_w, ada_b, norm_mix_g, norm_ffn_g, e_w_in, e_w_out, a_q_norm, a_k_norm, a_sink, b_gate_w_f, b_gate_b_f, b_gate_w_b, b_gate_b_b, b_out_norm, o_w_in, o_w_out, c_q_norm, c_k_norm, c_lambda_q1, c_lambda_k1, c_lambda_q2, c_lambda_k2, c_out_norm, p_w_q, p_sub_keys, p_u, p_v):
    cos, sin = rope_2d(x_sample.shape[1])

    xp = x_prompt
    Bp = xp.shape[0]
    ak_l, av_l, sf_l, sb_l, ck_l, cv_l = [], [], [], [], [], []
    for l in range(DEPTH):
        i = l // 2
        sh1, sc1, g1, sh2, sc2, g2 = modulation(c_ctx[None, :], ada_w[l], ada_b[l])
        h = rmsnorm(xp, norm_mix_g[l]) * (1.0 + sc1) + sh1
        if l % 2 == 0:
            aq, ak, av, bq, bk, bv, br, lgf, lgb = even_project(h, e_w_in[i], a_q_norm[i], a_k_norm[i], b_gate_w_f[i], b_gate_b_f[i], b_gate_w_b[i], b_gate_b_b[i])
            oa = attn_sink_dense(aq, ak, av, a_sink[i])
            s0 = jnp.zeros((Bp, B_HEADS, B_DK, B_DV), F32)
            ob, sf, sb = gla_bidir(bq, bk, bv, lgf, lgb, s0, s0)
            o = even_output(oa, ob, br, b_out_norm[i], e_w_out[i])
            ak_l.append(ak)
            av_l.append(av)
            sf_l.append(sf)
            sb_l.append(sb)
        else:
            lam_init = lambda_init(l)
            lam = diff_lambda(c_lambda_q1[i], c_lambda_k1[i], c_lambda_q2[i], c_lambda_k2[i], lam_init)
            q, k, v = odd_project(h, o_w_in[i], c_q_norm[i], c_k_norm[i])
            o = odd_output(diff_attention(q, k, v, lam), c_out_norm[i], lam_init, o_w_out[i])
            ck_l.append(k)
            cv_l.append(v)
        xp = xp + g1 * o
        h = rmsnorm(xp, norm_ffn_g[l]) * (1.0 + sc2) + sh2
        xp = xp + g2 * peer(h, p_w_q[l], p_sub_keys[l], p_u[l], p_v[l])

    xs = x_sample
    for l in range(DEPTH):
        i = l // 2
        sh1, sc1, g1, sh2, sc2, g2 = modulation(c, ada_w[l], ada_b[l])
        h = rmsnorm(xs, norm_mix_g[l]) * (1.0 + sc1) + sh1
        if l % 2 == 0:
            aq, ak, av, bq, bk, bv, br, lgf, lgb = even_project(h, e_w_in[i], a_q_norm[i], a_k_norm[i], b_gate_w_f[i], b_gate_b_f[i], b_gate_w_b[i], b_gate_b_b[i])
            aq = apply_rope_2d(aq, cos, sin)
            ak = apply_rope_2d(ak, cos, sin)
            oa = attn_window_ctx(aq, ak, av, cache_a_k[:, i].astype(ak.dtype), cache_a_v[:, i].astype(av.dtype), a_sink[i])
            ob, _, _ = gla_bidir(bq, bk, bv, lgf, lgb, state_b_fwd[:, i], state_b_bwd[:, i])
            o = even_output(oa, ob, br, b_out_norm[i], e_w_out[i])
        else:
            lam_init = lambda_init(l)
            lam = diff_lambda(c_lambda_q1[i], c_lambda_k1[i], c_lambda_q2[i], c_lambda_k2[i], lam_init)
            q, k, v = odd_project(h, o_w_in[i], c_q_norm[i], c_k_norm[i])
            q = apply_rope_2d(q, cos, sin)
            k = apply_rope_2d(k, cos, sin)
            k_all = jnp.concatenate([k, cache_c_k[:, i].astype(k.dtype)], axis=1)
            v_all = jnp.concatenate([v, cache_c_v[:, i].astype(v.dtype)], axis=1)
            o = odd_output(diff_attention(q, k_all, v_all, lam), c_out_norm[i], lam_init, o_w_out[i])
        xs = xs + g1 * o
        h = rmsnorm(xs, norm_ffn_g[l]) * (1.0 + sc2) + sh2
        xs = xs + g2 * peer(h, p_w_q[l], p_sub_keys[l], p_u[l], p_v[l])

    new_cache_a_k = jnp.stack(ak_l, axis=1)
    new_cache_a_v = jnp.stack(av_l, axis=1)
    new_state_b_fwd = jnp.stack(sf_l, axis=1)
    new_state_b_bwd = jnp.stack(sb_l, axis=1)
    new_cache_c_k = jnp.stack(ck_l, axis=1)
    new_cache_c_v = jnp.stack(cv_l, axis=1)
    return (xp, xs, new_cache_a_k, new_cache_a_v, new_state_b_fwd, new_state_b_bwd, new_cache_c_k, new_cache_c_v)
```

```python
import numpy as np
import math
from concourse.bass_utils import run_bass_kernel_spmd
import concourse.bass as bass
import concourse.mybir as mybir
from contextlib import ExitStack
F32 = mybir.dt.float32; BF16 = mybir.dt.bfloat16; U32 = mybir.dt.uint32; I32 = mybir.dt.int32; U16 = mybir.dt.uint16
AF = mybir.ActivationFunctionType
ALU = mybir.AluOpType
AX = mybir.AxisListType

class Buf:
    __slots__ = ("t", "w", "r", "name")
    def __init__(self, t, name):
        self.t = t; self.w = None; self.r = []; self.name = name
    def __getitem__(self, idx):
        return self.t[idx]

class Sched:
    SAME_ENGINE_SYNC = ("act", "dve", "pool")
    def __init__(self, nc, n_dma_sems=8):
        self.nc = nc
        self.es = ExitStack()
        self.eng = {"pe": nc.tensor, "act": nc.scalar, "dve": nc.vector, "pool": nc.gpsimd, "sp": nc.sync}
        self.sem = {k: self.es.enter_context(nc.semaphore("sem_" + k)) for k in self.eng}
        self.cnt = {k: 0 for k in self.eng}
        self.seen = {k: {} for k in self.eng}
        self.dma_sems = {}
        for q in ("sp", "pool", "act"):
            self.dma_sems[q] = [[self.es.enter_context(nc.semaphore(f"dsem_{q}{i}")), 0] for i in range(n_dma_sems)]
        self.dma_rr = {q: 0 for q in self.dma_sems}
        self.ninstr = 0
        self.out_tokens = []
    def sb(self, name, shape, dtype):
        self.uid = getattr(self, "uid", 0) + 1; name = f"{name}_{self.uid}"
        return Buf(self.es.enter_context(self.nc.sbuf_tensor(name, list(shape), dtype)), name)
    def ps(self, name, shape, dtype=F32):
        self.uid = getattr(self, "uid", 0) + 1; name = f"{name}_{self.uid}"
        return Buf(self.es.enter_context(self.nc.psum_tensor(name, list(shape), dtype)), name)
    def dram(self, name, shape, dtype, kind="Internal"):
        return Buf(self.nc.dram_tensor(name, list(shape), dtype, kind=kind).ap(), name)
    def _wait(self, e, tok):
        if tok is None: return
        sem, val, name = tok
        if self.seen[e].get(name, 0) >= val: return
        self.eng[e].wait_ge(sem, val)
        self.seen[e][name] = val
    def _deps(self, e, reads, writes):
        for b in reads:
            if b.w is not None: self._wait(e, b.w)
        for b in writes:
            if b.w is not None: self._wait(e, b.w)
            for t in b.r: self._wait(e, t)
    def _mark(self, tok, reads, writes):
        for b in reads:
            b.r = [t for t in b.r if t[2] != tok[2]] + [tok]
        for b in writes:
            b.w = tok; b.r = []
    def _collect(self, e, reads, writes):
        need = {}
        def add(tok):
            if tok is None: return
            sem, val, name = tok
            if self.seen[e].get(name, 0) >= val: return
            if name not in need or need[name][1] < val: need[name] = tok
        for b in reads:
            add(b.w)
        for b in writes:
            add(b.w)
            for t in b.r: add(t)
        return list(need.values())
    def op(self, e, fn, reads=(), writes=()):
        toks = self._collect(e, reads, writes)
        for tok in toks[:-1]:
            self._wait(e, tok)
        ins = fn(self.eng[e])
        if toks:
            sem, val, name = toks[-1]
            ins.wait_op(sem, val, "sem-ge")
            self.seen[e][name] = val
        self.cnt[e] += 1
        ins.then_inc(self.sem[e], 1)
        tok = (self.sem[e], self.cnt[e], "sem_" + e)
        if e not in self.SAME_ENGINE_SYNC:
            self.seen[e]["sem_" + e] = self.cnt[e]
        self._mark(tok, reads, writes)
        self.ninstr += 1
        return tok
    def dma(self, q, fn, reads=(), writes=()):
        for tok in self._collect(q, reads, writes):
            self._wait(q, tok)
        lst = self.dma_sems[q]
        i = self.dma_rr[q]; self.dma_rr[q] = (i + 1) % len(lst)
        sem, val = lst[i]
        name = f"dsem_{q}{i}"
        if val > 0:
            self._wait(q, (sem, val, name))
        ins = fn(self.eng[q])
        val += 16
        lst[i][1] = val
        ins.then_inc(sem, 16)
        tok = (sem, val, name)
        self._mark(tok, reads, writes)
        self.ninstr += 1
        return tok
    def barrier(self):
        names = list(self.eng)
        toks = []
        for q, lst in self.dma_sems.items():
            for i, (sem, val) in enumerate(lst):
                if val > 0: toks.append((sem, val, f"dsem_{q}{i}"))
        for e in names:
            if self.cnt[e] > 0: toks.append((self.sem[e], self.cnt[e], "sem_" + e))
        for e in names:
            for t in toks: self._wait(e, t)
    def phase_begin(self):
        self._saved_es = self.es
        self.es = ExitStack()
    def phase_end(self):
        self.barrier()
        self.es.close()
        self.es = self._saved_es
    def finish(self):
        for q, lst in self.dma_sems.items():
            for i, (sem, val) in enumerate(lst):
                if val > 0: self._wait("sp", (sem, val, f"dsem_{q}{i}"))
        for e in self.eng:
            if e != "sp" and self.cnt[e] > 0:
                self._wait("sp", (self.sem[e], self.cnt[e], "sem_" + e))
        self.es.close()

STOP = 4

def peer_consts(S, nc):
    c = {}
    c["iota16"] = S.sb("iota16", [128, 16], F32)
    c["iota16i"] = S.sb("iota16i", [128, 16], I32)
    S.op("pool", lambda e: e.iota(c["iota16i"][:], pattern=[[1, 16]], base=0, channel_multiplier=0), writes=[c["iota16i"]])
    S.op("dve", lambda e: e.tensor_copy(out=c["iota16"][:], in_=c["iota16i"][:]), reads=[c["iota16i"]], writes=[c["iota16"]])
    return c

def peer_alloc(S):
    B = {}
    B["qps"] = [S.ps(f"p_qps{i}", [128, 4, 128]) for i in range(2)]
    B["qT"] = S.sb("p_qT", [128, 8, 128], F32)
    B["sps"] = [S.ps(f"p_sps{i}", [128, 4, 128]) for i in range(4)]
    B["s1"] = S.sb("p_s1", [128, 16, 128], F32)
    B["s2"] = S.sb("p_s2", [128, 16, 128], F32)
    B["sv"] = S.sb("p_sv", [128, 16, 16], F32)
    B["si"] = S.sb("p_si", [128, 16, 16], U32)
    B["sif"] = S.sb("p_sif", [128, 16, 16], F32)
    B["cand"] = S.sb("p_cand", [128, 8, 256], F32)
    B["cand2"] = S.sb("p_cand2", [128, 8, 256], F32)
    B["fs"] = S.sb("p_fs", [128, 8, 16], F32)
    B["fi"] = S.sb("p_fi", [128, 8, 16], U32)
    B["au"] = S.sb("p_au", [128, 8, 16], U32)
    B["bu"] = S.sb("p_bu", [128, 8, 16], U32)
    B["af"] = S.sb("p_af", [128, 8, 16], F32)
    B["bf"] = S.sb("p_bf", [128, 8, 16], F32)
    B["eq"] = S.sb("p_eq", [128, 8, 16, 16], F32)
    B["If"] = S.sb("p_If", [128, 8, 16], F32)
    B["Jf"] = S.sb("p_Jf", [128, 8, 16], F32)
    B["Ef"] = S.sb("p_Ef", [128, 128], F32)
    B["eidx"] = S.sb("p_eidx", [128, 128], U32)
    B["ex"] = S.sb("p_ex", [128, 8, 16], F32)
    B["esum"] = S.sb("p_esum", [128, 8], F32)
    B["W"] = S.sb("p_W", [128, 128], F32)
    B["act"] = S.sb("p_act", [128, 128], F32)
    B["coef"] = S.sb("p_coef", [128, 128], F32)
    B["rows"] = []
    B["acc"] = None
    B["rr"] = 0
    return B

def top16(S, vals_in, vals_tmp, out_v, out_i, groups, bufs_r, bufs_w):
    for g in range(groups):
        S.op("dve", lambda e: e.max(out=out_v(g)[:, 0:8], in_=vals_in(g)), reads=bufs_r, writes=bufs_w)
        S.op("dve", lambda e: e.max_index(out=out_i(g)[:, 0:8], in_max=out_v(g)[:, 0:8], in_values=vals_in(g)), reads=bufs_r, writes=bufs_w)
        S.op("dve", lambda e: e.match_replace(out=vals_tmp(g), in_to_replace=out_v(g)[:, 0:8], in_values=vals_in(g), imm_value=-1e30), reads=bufs_r, writes=bufs_w)
        S.op("dve", lambda e: e.max(out=out_v(g)[:, 8:16], in_=vals_tmp(g)), reads=bufs_r, writes=bufs_w)
        S.op("dve", lambda e: e.max_index(out=out_i(g)[:, 8:16], in_max=out_v(g)[:, 8:16], in_values=vals_tmp(g)), reads=bufs_r, writes=bufs_w)

def peer_tile(S, B, C, h, hT32, wq, skT, u_dram, v_dram):
    for hh in range(8):
        qp = B["qps"][hh // 4]
        for dc in range(8):
            S.op("pe", lambda e: e.matmul(qp[:, hh % 4, :], lhsT=wq[:, dc, hh * 128:(hh + 1) * 128], rhs=hT32[:, dc, :],
                                          start=(dc == 0), stop=(dc == 7)), reads=[wq, hT32], writes=[qp])
    for k in range(2):
        S.op("act", lambda e: e.activation(out=B["qT"][:, 4 * k:4 * k + 4, :], in_=B["qps"][k][:], func=AF.Copy),
             reads=[B["qps"][k]], writes=[B["qT"]])
    if STOP<=-1: return B['acc']
    for g in range(16):
        hh, p = g // 2, g % 2
        sp = B["sps"][g // 4]
        S.op("pe", lambda e: e.matmul(sp[:, g % 4, :], lhsT=B["qT"][:, hh, :], rhs=skT[:, hh, p, :],
                                      start=True, stop=True), reads=[B["qT"], skT], writes=[sp])
    for k in range(4):
        S.op("act", lambda e: e.activation(out=B["s1"][:, 4 * k:4 * k + 4, :], in_=B["sps"][k][:], func=AF.Copy),
             reads=[B["sps"][k]], writes=[B["s1"]])
    if STOP<=0: return B['acc']
    top16(S, lambda g: B["s1"][:, g, :], lambda g: B["s2"][:, g, :], lambda g: B["sv"][:, g, :], lambda g: B["si"][:, g, :], 16,
          [B["s1"]], [B["s2"], B["sv"], B["si"]])
    if STOP<=1: return B['acc']
    svv = B["sv"][:].rearrange("p (h two) k -> p h two k", two=2)
    S.op("dve", lambda e: e.tensor_tensor(out=B["cand"][:].rearrange("p h (a b) -> p h a b", a=16),
                                          in0=svv[:, :, 0, :].unsqueeze(3).broadcast_to([128, 8, 16, 16]),
                                          in1=svv[:, :, 1, :].unsqueeze(2).broadcast_to([128, 8, 16, 16]), op=ALU.add),
         reads=[B["sv"]], writes=[B["cand"]])
    top16(S, lambda g: B["cand"][:, g, :], lambda g: B["cand2"][:, g, :], lambda g: B["fs"][:, g, :], lambda g: B["fi"][:, g, :], 8,
          [B["cand"]], [B["cand2"], B["fs"], B["fi"]])
    if STOP<=2: return B['acc']
    S.op("dve", lambda e: e.tensor_scalar(out=B["au"][:], in0=B["fi"][:], scalar1=4, scalar2=None, op0=ALU.logical_shift_right),
         reads=[B["fi"]], writes=[B["au"]])
    S.op("dve", lambda e: e.tensor_scalar(out=B["bu"][:], in0=B["fi"][:], scalar1=15, scalar2=None, op0=ALU.bitwise_and),
         reads=[B["fi"]], writes=[B["bu"]])
    S.op("dve", lambda e: e.tensor_copy(out=B["af"][:], in_=B["au"][:]), reads=[B["au"]], writes=[B["af"]])
    S.op("dve", lambda e: e.tensor_copy(out=B["bf"][:], in_=B["bu"][:]), reads=[B["bu"]], writes=[B["bf"]])
    S.op("dve", lambda e: e.tensor_copy(out=B["sif"][:], in_=B["si"][:]), reads=[B["si"]], writes=[B["sif"]])
    sifv = B["sif"][:].rearrange("p (h two) k -> p h two k", two=2)
    iota_b = C["iota16"][:].unsqueeze(1).unsqueeze(1).broadcast_to([128, 8, 16, 16])
    for (src, two, dst) in ((B["af"], 0, B["If"]), (B["bf"], 1, B["Jf"])):
        S.op("dve", lambda e: e.tensor_tensor(out=B["eq"][:], in0=iota_b, in1=src[:].unsqueeze(3).broadcast_to([128, 8, 16, 16]), op=ALU.is_equal),
             reads=[src, C["iota16"]], writes=[B["eq"]])
        S.op("dve", lambda e: e.tensor_tensor(out=B["eq"][:], in0=B["eq"][:], in1=sifv[:, :, two, :].unsqueeze(2).broadcast_to([128, 8, 16, 16]), op=ALU.mult),
             reads=[B["sif"]], writes=[B["eq"]])
        S.op("dve", lambda e: e.tensor_reduce(out=dst[:], in_=B["eq"][:], axis=AX.X, op=ALU.add), reads=[B["eq"]], writes=[dst])
    S.op("dve", lambda e: e.scalar_tensor_tensor(out=B["Ef"][:], in0=B["If"][:].rearrange("p h k -> p (h k)"), scalar=128.0, op0=ALU.mult,
                                                 in1=B["Jf"][:].rearrange("p h k -> p (h k)"), op1=ALU.add),
         reads=[B["If"], B["Jf"]], writes=[B["Ef"]])
    S.op("dve", lambda e: e.tensor_copy(out=B["eidx"][:], in_=B["Ef"][:]), reads=[B["Ef"]], writes=[B["eidx"]])
    if STOP<=3: return B['acc']
    S.op("dve", lambda e: e.tensor_tensor(out=B["ex"][:], in0=B["fs"][:], in1=B["fs"][:, :, 0:1].broadcast_to([128, 8, 16]), op=ALU.subtract),
         reads=[B["fs"]], writes=[B["ex"]])
    S.op("act", lambda e: e.activation(out=B["ex"][:], in_=B["ex"][:], func=AF.Exp), reads=[B["ex"]], writes=[B["ex"]])
    S.op("dve", lambda e: e.tensor_reduce(out=B["esum"][:], in_=B["ex"][:], axis=AX.X, op=ALU.add), reads=[B["ex"]], writes=[B["esum"]])
    S.op("dve", lambda e: e.reciprocal(out=B["esum"][:], in_=B["esum"][:]), reads=[B["esum"]], writes=[B["esum"]])
    S.op("dve", lambda e: e.tensor_tensor(out=B["W"][:].rearrange("p (h k) -> p h k", h=8), in0=B["ex"][:],
                                          in1=B["esum"][:].unsqueeze(2).broadcast_to([128, 8, 16]), op=ALU.mult),
         reads=[B["ex"], B["esum"]], writes=[B["W"]])
    if STOP<=4: return B['acc']
    nrow = len(B["rows"])
    for s in range(128):
        row = B["rows"][B["rr"] % nrow]; B["rr"] += 1
        S.dma("pool", lambda e: e.indirect_dma_start(out=row[:], out_offset=None, in_=u_dram[:, :],
                                                     in_offset=bass.IndirectOffsetOnAxis(ap=B["eidx"][:, s:s + 1], axis=0)),
              reads=[B["eidx"], u_dram], writes=[row])
        S.op("dve", lambda e: e.scalar_tensor_tensor(out=B["junk"][:], in0=row[:], scalar=1.0, op0=ALU.mult, in1=h[:], op1=ALU.mult,
                                                     accum_out=B["act"][:, s:s + 1]), reads=[row, h], writes=[B["junk"], B["act"]])
    if STOP<=5: return B['acc']
    S.op("act", lambda e: e.activation(out=B["coef"][:], in_=B["act"][:], func=AF.Gelu), reads=[B["act"]], writes=[B["coef"]])
    S.op("dve", lambda e: e.tensor_tensor(out=B["coef"][:], in0=B["coef"][:], in1=B["W"][:], op=ALU.mult), reads=[B["W"]], writes=[B["coef"]])
    for s in range(128):
        row = B["rows"][B["rr"] % nrow]; B["rr"] += 1
        S.dma("pool", lambda e: e.indirect_dma_start(out=row[:], out_offset=None, in_=v_dram[:, :],
                                                     in_offset=bass.IndirectOffsetOnAxis(ap=B["eidx"][:, s:s + 1], axis=0)),
              reads=[B["eidx"], v_dram], writes=[row])
        if s == 0:
            S.op("dve", lambda e: e.tensor_scalar(out=B["acc"][:], in0=row[:], scalar1=B["coef"][:, 0:1], scalar2=None, op0=ALU.mult),
                 reads=[row, B["coef"]], writes=[B["acc"]])
        else:
            S.op("dve", lambda e: e.scalar_tensor_tensor(out=B["acc"][:], in0=row[:], scalar=B["coef"][:, s:s + 1], op0=ALU.mult,
                                                         in1=B["acc"][:], op1=ALU.add), reads=[row, B["coef"]], writes=[B["acc"]])
    return B["acc"]

NTP, NTS = 8, 32
NT = NTP + NTS
T = NT * 128
DEPTH = 4
EPS = 1e-6
PHASES = 4
DUMP = False
GSTOP = 99
GSUB = 99

def tile_set(t):
    return 0 if t < NTP else 1

def build_program(n_layers=DEPTH, do_mixer=True, do_peer=True):
    nc = bass.Bass("TRN2", target_bir_lowering=False)
    S = Sched(nc)
    D = {}
    def din(name, shape, dt=F32):
        D[name] = S.dram(name, shape, dt, kind="ExternalInput"); return D[name]
    def dout(name, shape, dt=F32):
        D[name] = S.dram(name, shape, dt, kind="ExternalOutput"); return D[name]
    xin = din("xin", [T, 1024]); cT = din("cT", [128, 8, 2])
    ada_w = din("ada_w", [4, 1024, 6144]); ada_b = din("ada_b", [4, 6144])
    g_mix = din("norm_mix_g", [4, 1024]); g_ffn = din("norm_ffn_g", [4, 1024])
    p_w_q = din("p_w_q", [4, 1024, 1024]); skz = din("skz", [4, 128, 8, 2, 128])
    p_uT = [din(f"p_uT{l}", [64, 128, 2048]) for l in range(4)]
    p_v = [din(f"p_v{l}", [64, 128, 2048]) for l in range(4)]
    ident_d = din("ident", [128, 128])
    yout = dout("yout", [T, 1024])
    xbuf = [S.dram(f"xbuf{i}", [T, 1024], F32) for i in range(2)]
    nstage = 2 * n_layers
    def res(stage):
        if stage == 0: return xin
        if stage == nstage: return yout
        return xbuf[stage % 2]
    ident = S.sb("ident_sb", [128, 128], F32)
    S.dma("sp", lambda e: e.dma_start(out=ident[:], in_=ident_d[:, :]), reads=[ident_d], writes=[ident])
    PC = peer_consts(S, nc)
    mod = [S.sb(f"mod{s}", [128, 6144], F32) for s in range(2)]
    crep = [S.sb(f"crep{s}", [128, 8, 128], F32) for s in range(2)]
    csil = S.sb("csil", [128, 8, 2], F32)
    S.dma("sp", lambda e: e.dma_start(out=csil[:], in_=cT[:, :, :]), reads=[cT], writes=[csil])
    S.op("act", lambda e: e.activation(out=csil[:], in_=csil[:], func=AF.Silu), reads=[csil], writes=[csil])
    for s in range(2):
        S.op("dve", lambda e: e.tensor_copy(out=crep[s][:], in_=csil[:, :, s:s + 1].broadcast_to([128, 8, 128])), reads=[csil], writes=[crep[s]])

    def modulation(l):
        S.phase_begin()
        wch = [S.sb(f"m_wch{i}", [128, 8, 256], F32) for i in range(2)]
        bch = [S.sb(f"m_bch{i}", [128, 256], F32) for i in range(2)]
        gbc = [S.sb(f"m_gbc{i}", [128, 1024], F32) for i in range(2)]
        mps = [S.ps(f"m_ps{i}", [128, 256]) for i in range(2)]
        S.dma("sp", lambda e: e.dma_start(out=gbc[0][:], in_=g_mix[l, :].partition_broadcast(128)), reads=[g_mix], writes=[gbc[0]])
        S.dma("sp", lambda e: e.dma_start(out=gbc[1][:], in_=g_ffn[l, :].partition_broadcast(128)), reads=[g_ffn], writes=[gbc[1]])
        k = 0
        for n in range(24):
            w = wch[n % 2]; b = bch[n % 2]
            S.dma("sp", lambda e: e.dma_start(out=w[:], in_=ada_w[l, :, n * 256:(n + 1) * 256].rearrange("(dc p) n -> p dc n", p=128)), reads=[ada_w], writes=[w])
            S.dma("sp", lambda e: e.dma_start(out=b[:], in_=ada_b[l, n * 256:(n + 1) * 256].partition_broadcast(128)), reads=[ada_b], writes=[b])
            for s in range(2):
                p = mps[k % 2]; k += 1
                for dc in range(8):
                    S.op("pe", lambda e: e.matmul(p[:], lhsT=crep[s][:, dc, :], rhs=w[:, dc, :], start=(dc == 0), stop=(dc == 7)), reads=[crep[s], w], writes=[p])
                S.op("dve", lambda e: e.tensor_tensor(out=mod[s][:, n * 256:(n + 1) * 256], in0=p[:], in1=b[:], op=ALU.add), reads=[p, b], writes=[mod[s]])
        for s in range(2):
            for (j, gi) in ((1, 0), (4, 1)):
                S.op("dve", lambda e: e.scalar_tensor_tensor(out=mod[s][:, j * 1024:(j + 1) * 1024], in0=mod[s][:, j * 1024:(j + 1) * 1024], scalar=1.0, op0=ALU.add,
                                                             in1=gbc[gi][:], op1=ALU.mult), reads=[gbc[gi]], writes=[mod[s]])
        S.phase_end()

    def norm_tile(NB, x, s, which, want_T32=True):
        sh = mod[s][:, (3 * which) * 1024:(3 * which + 1) * 1024]
        scp = mod[s][:, (3 * which + 1) * 1024:(3 * which + 2) * 1024]
        S.op("act", lambda e: e.activation(out=NB["junk"][:], in_=x[:], func=AF.Square, accum_out=NB["ss"][:]), reads=[x], writes=[NB["junk"], NB["ss"]])
        S.op("dve", lambda e: e.tensor_scalar(out=NB["ss"][:], in0=NB["ss"][:], scalar1=1.0 / 1024, scalar2=EPS, op0=ALU.mult, op1=ALU.add), reads=[NB["ss"]], writes=[NB["ss"]])
        S.op("act", lambda e: e.activation(out=NB["ss"][:], in_=NB["ss"][:], func=AF.Sqrt), reads=[NB["ss"]], writes=[NB["ss"]])
        S.op("dve", lambda e: e.reciprocal(out=NB["ss"][:], in_=NB["ss"][:]), reads=[NB["ss"]], writes=[NB["ss"]])
        S.op("dve", lambda e: e.scalar_tensor_tensor(out=NB["h"][:], in0=x[:], scalar=NB["ss"][:, 0:1], op0=ALU.mult, in1=scp, op1=ALU.mult), reads=[x, NB["ss"], mod[s]], writes=[NB["h"]])
        S.op("pool", lambda e: e.tensor_tensor(out=NB["h"][:], in0=NB["h"][:], in1=sh, op=ALU.add), reads=[mod[s]], writes=[NB["h"]])
        if want_T32:
            for dc in range(8):
                p = NB["tps"][dc // 4]
                S.op("pe", lambda e: e.transpose(p[:, dc % 4, :], NB["h"][:, dc * 128:(dc + 1) * 128], ident[:]), reads=[NB["h"], ident], writes=[p])
            for k in range(2):
                S.op("act", lambda e: e.activation(out=NB["hT32"][:, 4 * k:4 * k + 4, :], in_=NB["tps"][k][:], func=AF.Copy), reads=[NB["tps"][k]], writes=[NB["hT32"]])

    def norm_alloc(pfx):
        NB = {}
        NB["junk"] = S.sb(pfx + "junk", [128, 1024], F32); NB["ss"] = S.sb(pfx + "ss", [128, 1], F32)
        NB["h"] = S.sb(pfx + "h", [128, 1024], F32); NB["hT32"] = S.sb(pfx + "hT32", [128, 8, 128], F32)
        NB["tps"] = [S.ps(pfx + f"tps{i}", [128, 4, 128]) for i in range(2)]
        return NB

    IJW_d = S.dram("IJW_d", [T, 3, 128], F32); hTb_d = S.dram("hTb_d", [128, 8, T], BF16)
    iota128 = S.sb("iota128", [128, 128], F32); iota128i = S.sb("iota128i", [128, 128], I32)
    S.op("pool", lambda e: e.iota(iota128i[:], pattern=[[1, 128]], base=0, channel_multiplier=0), writes=[iota128i])
    S.op("dve", lambda e: e.tensor_copy(out=iota128[:], in_=iota128i[:]), reads=[iota128i], writes=[iota128])

    def peer_route(l, st_in):
        S.phase_begin()
        B = peer_alloc(S); NB = norm_alloc("pn_")
        wq = S.sb("pl_wq", [128, 8, 1024], F32); sk = S.sb("pl_sk", [128, 8, 2, 128], F32)
        xt = S.sb("pl_x", [128, 1024], F32); hTb = S.sb("pl_hTb", [128, 8, 128], BF16)
        S.dma("sp", lambda e: e.dma_start(out=wq[:], in_=p_w_q[l, :, :].rearrange("(dc p) n -> p dc n", p=128)), reads=[p_w_q], writes=[wq])
        S.dma("sp", lambda e: e.dma_start(out=sk[:], in_=skz[l, :, :, :, :]), reads=[skz], writes=[sk])
        xi = res(st_in)
        for t in range(NT):
            s = tile_set(t); r0 = t * 128
            S.dma("sp", lambda e: e.dma_start(out=xt[:], in_=xi[r0:r0 + 128, :]), reads=[xi], writes=[xt])
            norm_tile(NB, xt, s, 1)
            S.op("pool", lambda e: e.tensor_copy(out=hTb[:], in_=NB["hT32"][:]), reads=[NB["hT32"]], writes=[hTb])
            S.dma("sp", lambda e: e.dma_start(out=hTb_d[:, :, r0:r0 + 128], in_=hTb[:]), reads=[hTb], writes=[hTb_d])
            peer_tile(S, B, PC, NB["h"], NB["hT32"], wq, sk, None, None)
            S.dma("sp", lambda e: e.dma_start(out=IJW_d[r0:r0 + 128, 0, :], in_=B["If"][:].rearrange("p h k -> p (h k)")), reads=[B["If"]], writes=[IJW_d])
            S.dma("sp", lambda e: e.dma_start(out=IJW_d[r0:r0 + 128, 1, :], in_=B["Jf"][:].rearrange("p h k -> p (h k)")), reads=[B["Jf"]], writes=[IJW_d])
            S.dma("sp", lambda e: e.dma_start(out=IJW_d[r0:r0 + 128, 2, :], in_=B["W"][:]), reads=[B["W"]], writes=[IJW_d])
        S.phase_end()

    def peer_dense(l, st_in, st_out):
        S.phase_begin()
        GRP = 2
        pb = [S.ps(f"pd_pb{k}", [128, 4, 128]) for k in range(8)]
        flat = lambda b_: b_[:].rearrange("p a b -> p (a b)")
        xt = [S.sb(f"pd_x{k}", [128, 1024], F32) for k in range(GRP)]
        hTg = S.sb("pd_hTg", [128, 8, GRP * 128], BF16)
        ijw = S.sb("pd_ijw", [128, 3, 128], F32); ijwT = S.sb("pd_ijwT", [128, 3, 128], F32)
        Ab = [S.sb(f"pd_A{k}", [128, 8, 128], BF16) for k in range(2)]
        Bb = [S.sb(f"pd_B{k}", [128, 8, 128], BF16) for k in range(2)]
        GT = S.sb("pd_GT", [128, 128, GRP * 128], BF16)
        EC = 256; NSC = 16384 // EC; CPS = EC // 128
        u32 = [S.sb(f"pd_u32{k}", [128, 8, EC], F32) for k in range(2)]; ub = [S.sb(f"pd_ub{k}", [128, 8, EC], BF16) for k in range(2)]
        v32 = [S.sb(f"pd_v32{k}", [128, CPS, 1024], F32) for k in range(2)]; vb = [S.sb(f"pd_vb{k}", [128, CPS, 1024], BF16) for k in range(2)]
        gs = [S.sb(f"pd_gs{k}", [128, GRP * 128], F32) for k in range(2)]; ga = [S.sb(f"pd_ga{k}", [128, GRP * 128], BF16) for k in range(2)]
        xo = S.sb("pd_xo", [128, 1024], F32)
        xi, xod = res(st_in), res(st_out)
        uT_d, v_d = p_uT[l], p_v[l]
        for g0 in range(0, NT, GRP):
            S.dma("sp", lambda e: e.dma_start(out=hTg[:], in_=hTb_d[:, :, g0 * 128:(g0 + GRP) * 128]), reads=[hTb_d], writes=[hTg])
            for k in range(GRP):
                t = g0 + k; r0 = t * 128
                S.dma("sp", lambda e: e.dma_start(out=xt[k][:], in_=xi[r0:r0 + 128, :]), reads=[xi], writes=[xt[k]])
                S.dma("sp", lambda e: e.dma_start(out=ijw[:], in_=IJW_d[r0:r0 + 128, :, :]), reads=[IJW_d], writes=[ijw])
                for a in range(3):
                    S.op("pe", lambda e: e.transpose(pb[0][:, a, :], ijw[:, a, :], ident[:]), reads=[ijw, ident], writes=[pb[0]])
                S.op("act", lambda e: e.activation(out=ijwT[:], in_=pb[0][:, 0:3, :], func=AF.Copy), reads=[pb[0]], writes=[ijwT])
                for q8 in range(16):
                    A = Ab[q8 % 2]; Bm = Bb[q8 % 2]
                    tok0 = q8 * 8
                    io8 = iota128[:].unsqueeze(1).broadcast_to([128, 8, 128])
                    S.op("dve", lambda e: e.tensor_tensor(out=A[:], in0=io8, in1=ijwT[:, 0, tok0:tok0 + 8].unsqueeze(2).broadcast_to([128, 8, 128]), op=ALU.is_equal),
                         reads=[iota128, ijwT], writes=[A])
                    S.op("dve", lambda e: e.tensor_tensor(out=A[:], in0=A[:], in1=ijwT[:, 2, tok0:tok0 + 8].unsqueeze(2).broadcast_to([128, 8, 128]), op=ALU.mult),
                         reads=[ijwT], writes=[A])
                    S.op("dve", lambda e: e.tensor_tensor(out=Bm[:], in0=io8, in1=ijwT[:, 1, tok0:tok0 + 8].unsqueeze(2).broadcast_to([128, 8, 128]), op=ALU.is_equal),
                         reads=[iota128, ijwT], writes=[Bm])
                    for hq in range(2):
                        gp = pb[1 + hq]
                        for tt in range(4):
                            S.op("pe", lambda e: e.matmul(gp[:, tt, :], lhsT=Bm[:, hq * 4 + tt, :], rhs=A[:, hq * 4 + tt, :], start=True, stop=True), reads=[A, Bm], writes=[gp])
                        c0 = k * 128 + tok0 + hq * 4
                        S.op("act", lambda e: e.activation(out=GT[:, :, c0:c0 + 4], in_=gp[:].rearrange("p t i -> p i t"), func=AF.Copy), reads=[gp], writes=[GT])
            for sc in range(NSC):
                e0 = sc * EC
                U32 = u32[sc % 2]; UB = ub[sc % 2]; V32 = v32[sc % 2]; VB = vb[sc % 2]
                S.dma("sp", lambda e: e.dma_start(out=U32[:].rearrange("p a b -> p (a b)"), in_=uT_d[sc, :, :]), reads=[uT_d], writes=[U32])
                S.dma("sp", lambda e: e.dma_start(out=V32[:].rearrange("p a b -> p (a b)"), in_=v_d[sc, :, :]), reads=[v_d], writes=[V32])
                S.op("pool", lambda e: e.tensor_copy(out=UB[:], in_=U32[:]), reads=[U32], writes=[UB])
                S.op("act", lambda e: e.activation(out=VB[:], in_=V32[:], func=AF.Copy), reads=[V32], writes=[VB])
                for ch in range(CPS):
                    i_ = sc * CPS + ch
                    sp_ = pb[2 + (i_ % 2)]
                    for dc in range(8):
                        S.op("pe", lambda e: e.matmul(flat(sp_)[:, 0:GRP * 128], lhsT=UB[:, dc, ch * 128:(ch + 1) * 128], rhs=hTg[:, dc, :], start=(dc == 0), stop=(dc == 7)), reads=[UB, hTg], writes=[sp_])
                    G_ = gs[i_ % 2]; GA = ga[i_ % 2]
                    S.op("act", lambda e: e.activation(out=G_[:], in_=flat(sp_)[:, 0:GRP * 128], func=AF.Gelu), reads=[sp_], writes=[G_])
                    S.op("dve", lambda e: e.tensor_tensor(out=GA[:], in0=G_[:], in1=GT[:, i_, :], op=ALU.mult), reads=[G_, GT], writes=[GA])
                    for k in range(GRP):
                        for hf in range(2):
                            acc = pb[4 + k * 2 + hf]
                            S.op("pe", lambda e: e.matmul(flat(acc), lhsT=GA[:, k * 128:(k + 1) * 128], rhs=VB[:, ch, hf * 512:(hf + 1) * 512], start=(i_ == 0), stop=(i_ == 127)), reads=[GA, VB], writes=[acc])
            for k in range(GRP):
                t = g0 + k; r0 = t * 128; s = tile_set(t)
                for hf in range(2):
                    S.op("dve", lambda e: e.tensor_tensor(out=xo[:, hf * 512:(hf + 1) * 512], in0=flat(pb[4 + k * 2 + hf]), in1=mod[s][:, 5 * 1024 + hf * 512:5 * 1024 + (hf + 1) * 512], op=ALU.mult),
                         reads=[pb[4 + k * 2 + hf], mod[s]], writes=[xo])
                S.op("pool", lambda e: e.tensor_tensor(out=xo[:], in0=xo[:], in1=xt[k][:], op=ALU.add), reads=[xt[k]], writes=[xo])
                S.dma("sp", lambda e: e.dma_start(out=xod[r0:r0 + 128, :], in_=xo[:]), reads=[xo], writes=[xod])
        S.phase_end()

    def peer_layer(l, st_in, st_out):
        peer_route(l, st_in)
        peer_dense(l, st_in, st_out)

    NSP = NTP // 2
    seqs = [(2 * k, 2, 0, False, k) for k in range(NSP)] + [(NTP, NTS, 1, True, 0)]
    e_w_in = din("e_w_in", [2, 1024, 2336]); e_w_out = din("e_w_out", [2, 1024, 1024])
    o_w_in = din("o_w_in", [2, 1024, 3072]); o_w_out = din("o_w_out", [2, 1024, 1024])
    a_q_norm = din("a_q_norm", [2, 64]); a_k_norm = din("a_k_norm", [2, 64]); a_sink = din("a_sink", [2, 8])
    gw_f = din("b_gate_w_f", [2, 16, 256]); gb_f = din("b_gate_b_f", [2, 256])
    gw_b = din("b_gate_w_b", [2, 16, 256]); gb_b = din("b_gate_b_b", [2, 256])
    b_on = din("b_out_norm", [2, 128])
    c_q_norm = din("c_q_norm", [2, 64]); c_k_norm = din("c_k_norm", [2, 64]); c_on = din("c_out_norm", [2, 128])
    lqk = [din(n, [2, 64]) for n in ("c_lambda_q1", "c_lambda_k1", "c_lambda_q2", "c_lambda_k2")]
    cak = din("cak", [2, 512, 128]); cav = din("cav", [2, 512, 128])
    sbf = din("sbf", [2, 4, 64, 128]); sbb = din("sbb", [2, 4, 64, 128])
    cck = din("cck", [2, 512, 1024]); ccv = din("ccv", [2, 512, 1024])
    rope_d = din("rope", [NTS * 128, 64]); tri_d = din("tri", [2, 128, 128])
    nak = dout("nak", [NSP, 2, 256, 128]); nav = dout("nav", [NSP, 2, 256, 128])
    nsf = dout("nsf", [NSP, 2, 4, 64, 128]); nsb = dout("nsb", [NSP, 2, 4, 64, 128])
    nck = dout("nck", [NSP, 2, 256, 1024]); ncv = dout("ncv", [NSP, 2, 256, 1024])
    TX = T + 512
    PQ_d = S.dram("PQ_d", [T, 1024], F32); KT_d = S.dram("KT_d", [128, 8, TX], BF16)
    VXe_d = S.dram("VXe_d", [TX, 2, 65], BF16); VXo_d = S.dram("VXo_d", [TX, 8, 129], BF16)
    GL_d = S.dram("GL_d", [T, 1536], F32, kind=("ExternalOutput" if DUMP else "Internal")); LG_d = S.dram("LG_d", [T, 512], F32, kind=("ExternalOutput" if DUMP else "Internal"))
    OB_d = [S.dram(f"OB_d{k}", [T, 512], F32) for k in range(2)]
    OA_d = S.dram("OA_d", [T, 1024], F32)
    tri = S.sb("tri_sb", [128, 2, 128], F32)
    S.dma("sp", lambda e: e.dma_start(out=tri[:], in_=tri_d[:, :, :].rearrange("k p n -> p k n")), reads=[tri_d], writes=[tri])
    ones_f = S.sb("ones_f", [128, 128], F32)
    S.op("dve", lambda e: e.memset(ones_f[:], 1.0), writes=[ones_f])

    def rms_groups(X, ng, sq, ss, gains):
        xv = X.rearrange("p (g d) -> p g d", d=64)
        S.op("dve", lambda e: e.tensor_tensor(out=sq[:, 0:ng * 64], in0=X, in1=X, op=ALU.mult), reads=[Xb[0]], writes=[sq])
        S.op("dve", lambda e: e.tensor_reduce(out=ss[:, 0:ng], in_=sq[:, 0:ng * 64].rearrange("p (g d) -> p g d", d=64), axis=AX.X, op=ALU.add), reads=[sq], writes=[ss])
        S.op("dve", lambda e: e.tensor_scalar(out=ss[:, 0:ng], in0=ss[:, 0:ng], scalar1=1.0 / 64, scalar2=EPS, op0=ALU.mult, op1=ALU.add), reads=[ss], writes=[ss])
        S.op("act", lambda e: e.activation(out=ss[:, 0:ng], in_=ss[:, 0:ng], func=AF.Sqrt), reads=[ss], writes=[ss])
        S.op("dve", lambda e: e.reciprocal(out=ss[:, 0:ng], in_=ss[:, 0:ng]), reads=[ss], writes=[ss])
        S.op("dve", lambda e: e.tensor_tensor(out=xv, in0=xv, in1=ss[:, 0:ng].unsqueeze(2).broadcast_to([128, ng, 64]), op=ALU.mult), reads=[ss], writes=[Xb[0]])
        for (g0, g1, bc) in gains:
            S.op("dve", lambda e: e.tensor_tensor(out=xv[:, g0:g1, :], in0=xv[:, g0:g1, :], in1=bc[:].unsqueeze(1).broadcast_to([128, g1 - g0, 64]), op=ALU.mult), reads=[bc], writes=[Xb[0]])
    Xb = [None]

    def proj_phase(l, st_in):
        even = (l % 2 == 0); i = l // 2
        NC = 2336 if even else 3072
        NG = 10 if even else 32
        NQ = 8 if even else 16
        NK = NG - NQ
        dv = 64 if even else 128
        nkt = 2 if even else 8
        S.phase_begin()
        NB = norm_alloc("pj_")
        hTb = S.sb("pj_hTb", [128, 8, 128], BF16)
        W = S.sb("pj_W", [128, 8, NC], BF16)
        w_d = e_w_in if even else o_w_in
        hc = NC // 2
        for dc in range(8):
            for k in range(2):
                S.dma("pool", lambda e: e.dma_start(out=W[:, dc, k * hc:(k + 1) * hc], in_=w_d[i, dc * 128:(dc + 1) * 128, k * hc:(k + 1) * hc]), reads=[w_d], writes=[W])
        P = S.sb("pj_P", [128, NC], F32); Xb[0] = P
        pps = [S.ps(f"pj_pps{k}", [128, 512]) for k in range(2)]
        xt = S.sb("pj_x", [128, 1024], F32)
        qn_bc = S.sb("pj_qn", [128, 64], F32); kn_bc = S.sb("pj_kn", [128, 64], F32)
        S.dma("sp", lambda e: e.dma_start(out=qn_bc[:], in_=(a_q_norm if even else c_q_norm)[i, :].partition_broadcast(128)), writes=[qn_bc])
        S.dma("sp", lambda e: e.dma_start(out=kn_bc[:], in_=(a_k_norm if even else c_k_norm)[i, :].partition_broadcast(128)), writes=[kn_bc])
        sq = S.sb("pj_sq", [128, NG * 64], F32); ss = S.sb("pj_ss", [128, NG], F32)
        cs = S.sb("pj_cs", [128, 64], F32)
        rt = [S.sb(f"pj_rt{k}", [128, NG, 2, 16], F32) for k in range(4)]
        qkr = S.sb("pj_qkr", [128, NG * 64], F32)
        kps = [S.ps(f"pj_kps{k}", [128, 4, 128]) for k in range(2)]
        KTb = S.sb("pj_KTb", [128, 8, 128], BF16)
        Vxb = S.sb("pj_Vxb", [128, (2 if even else 8), dv + 1], BF16)
        S.op("dve", lambda e: e.memset(Vxb[:], 1.0), writes=[Vxb])
        ksrc = S.sb("pj_ksrc", [128, 1024], F32)
        craw = S.sb("pj_craw", [128, 1024], F32)
        if even:
            bgpad = S.sb("pj_bgpad", [128, 128], F32); S.op("dve", lambda e: e.memset(bgpad[:], 0.0), writes=[bgpad])
            bgT = S.sb("pj_bgT", [128, 128], F32)
            bgps = S.ps("pj_bgps", [128, 128]); gps = S.ps("pj_gps", [128, 512])
            gwfb = S.sb("pj_gwfb", [128, 512], F32); S.op("dve", lambda e: e.memset(gwfb[:], 0.0), writes=[gwfb])
            S.dma("sp", lambda e: e.dma_start(out=gwfb[0:16, 0:256], in_=gw_f[i, :, :]), writes=[gwfb])
            S.dma("sp", lambda e: e.dma_start(out=gwfb[16:32, 256:512], in_=gw_b[i, :, :]), writes=[gwfb])
            gb_bc = S.sb("pj_gb", [128, 512], F32)
            S.dma("sp", lambda e: e.dma_start(out=gb_bc[:, 0:256], in_=gb_f[i, :].partition_broadcast(128)), writes=[gb_bc])
            S.dma("sp", lambda e: e.dma_start(out=gb_bc[:, 256:512], in_=gb_b[i, :].partition_broadcast(128)), writes=[gb_bc])
            lg = S.sb("pj_lg", [128, 512], F32)

        def emit_kv(kbuf, k_ap, vbuf, v_ap, col0):
            if even:
                kv3 = k_ap.rearrange("p (kv d) -> p kv d", d=64)
                for dup in range(2):
                    S.op("pool", lambda e: e.tensor_copy(out=ksrc[:, 0:256].rearrange("p (kv u d) -> p kv u d", kv=2, u=2)[:, :, dup, :], in_=kv3), reads=[kbuf], writes=[ksrc])
                src = ksrc
            else:
                src = kbuf
            for k in range(nkt):
                p = kps[k // 4]
                sap = (ksrc[:, k * 128:(k + 1) * 128] if even else k_ap[:, k * 128:(k + 1) * 128])
                S.op("pe", lambda e: e.transpose(p[:, k % 4, :], sap, ident[:]), reads=[src, ident], writes=[p])
            for k in range((nkt + 3) // 4):
                n = min(4, nkt - 4 * k)
                S.op("act", lambda e: e.activation(out=KTb[:, 4 * k:4 * k + n, :], in_=kps[k][:, 0:n, :], func=AF.Copy), reads=[kps[k]], writes=[KTb])
            S.dma("sp", lambda e: e.dma_start(out=KT_d[:, 0:nkt, col0:col0 + 128], in_=KTb[:, 0:nkt, :]), reads=[KTb], writes=[KT_d])
            nv = 2 if even else 8
            S.op("act", lambda e: e.activation(out=Vxb[:, :, 0:dv], in_=v_ap.rearrange("p (h d) -> p h d", d=dv), func=AF.Copy), reads=[vbuf], writes=[Vxb])
            vx_d = VXe_d if even else VXo_d
            S.dma("sp", lambda e: e.dma_start(out=vx_d[col0:col0 + 128, :, :], in_=Vxb[:]), reads=[Vxb], writes=[vx_d])

        xi = res(st_in)
        for (t0, nt, s, is_s, bl) in seqs:
            for n in range(nt):
                t = t0 + n; r0 = t * 128
                S.dma("sp", lambda e: e.dma_start(out=xt[:], in_=xi[r0:r0 + 128, :]), reads=[xi], writes=[xt])
                norm_tile(NB, xt, s, 0, want_T32=False)
                for dc in range(8):
                    p = NB["tps"][dc // 4]
                    S.op("pe", lambda e: e.transpose(p[:, dc % 4, :], NB["h"][:, dc * 128:(dc + 1) * 128], ident[:]), reads=[NB["h"], ident], writes=[p])
                for k in range(2):
                    S.op("act", lambda e: e.activation(out=hTb[:, 4 * k:4 * k + 4, :], in_=NB["tps"][k][:], func=AF.Copy), reads=[NB["tps"][k]], writes=[hTb])
                nch = (NC + 511) // 512
                for c in range(nch):
                    c0 = c * 512; cw = min(512, NC - c0); p = pps[c % 2]
                    for dc in range(8):
                        S.op("pe", lambda e: e.matmul(p[:, 0:cw], lhsT=hTb[:, dc, :], rhs=W[:, dc, c0:c0 + cw], start=(dc == 0), stop=(dc == 7)), reads=[hTb, W], writes=[p])
                    S.op("act", lambda e: e.activation(out=P[:, c0:c0 + cw], in_=p[:, 0:cw], func=AF.Copy), reads=[p], writes=[P])
                QK = P[:, 0:NG * 64]
                rms_groups(QK, NG, sq, ss, [(0, NQ, qn_bc), (NQ, NG, kn_bc)])
                voff = NG * 64
                if not is_s:
                    ko, vo = (nak, nav) if even else (nck, ncv)
                    S.dma("sp", lambda e: e.dma_start(out=ko[bl, i, n * 128:(n + 1) * 128, :], in_=P[:, NQ * 64:NG * 64]), reads=[P], writes=[ko])
                    S.dma("sp", lambda e: e.dma_start(out=vo[bl, i, n * 128:(n + 1) * 128, :], in_=P[:, voff:voff + NK * 64]), reads=[P], writes=[vo])
                    qsrc = P; qk_ap = QK
                else:
                    S.dma("sp", lambda e: e.dma_start(out=cs[:], in_=rope_d[n * 128:(n + 1) * 128, :]), reads=[rope_d], writes=[cs])
                    x5 = QK.rearrange("p (g a b d) -> p g a b d", a=2, b=2, d=16)
                    o5 = qkr[:].rearrange("p (g a b d) -> p g a b d", a=2, b=2, d=16)
                    x1, x2 = x5[:, :, :, 0, :], x5[:, :, :, 1, :]
                    cosb = cs[:, 0:32].rearrange("p (a d) -> p a d", a=2).unsqueeze(1).broadcast_to([128, NG, 2, 16])
                    sinb = cs[:, 32:64].rearrange("p (a d) -> p a d", a=2).unsqueeze(1).broadcast_to([128, NG, 2, 16])
                    S.op("dve", lambda e: e.tensor_tensor(out=rt[0][:], in0=x1, in1=cosb, op=ALU.mult), reads=[P, cs], writes=[rt[0]])
                    S.op("pool", lambda e: e.tensor_tensor(out=rt[1][:], in0=x2, in1=sinb, op=ALU.mult), reads=[P, cs], writes=[rt[1]])
                    S.op("dve", lambda e: e.tensor_tensor(out=rt[2][:], in0=x2, in1=cosb, op=ALU.mult), reads=[P, cs], writes=[rt[2]])
                    S.op("pool", lambda e: e.tensor_tensor(out=rt[3][:], in0=x1, in1=sinb, op=ALU.mult), reads=[P, cs], writes=[rt[3]])
                    S.op("dve", lambda e: e.tensor_tensor(out=o5[:, :, :, 0, :], in0=rt[0][:], in1=rt[1][:], op=ALU.subtract), reads=[rt[0], rt[1]], writes=[qkr])
                    S.op("dve", lambda e: e.tensor_tensor(out=o5[:, :, :, 1, :], in0=rt[2][:], in1=rt[3][:], op=ALU.add), reads=[rt[2], rt[3]], writes=[qkr])
                    qsrc = qkr; qk_ap = qkr[:]
                S.dma("sp", lambda e: e.dma_start(out=PQ_d[r0:r0 + 128, 0:NQ * 64], in_=qk_ap[:, 0:NQ * 64]), reads=[qsrc], writes=[PQ_d])
                emit_kv(qsrc, qk_ap[:, NQ * 64:NG * 64], P, P[:, voff:voff + NK * 64], r0)
                if even:
                    S.dma("sp", lambda e: e.dma_start(out=GL_d[r0:r0 + 128, :], in_=P[:, 768:2304]), reads=[P], writes=[GL_d])
                    S.op("pool", lambda e: e.tensor_copy(out=bgpad[:, 0:32], in_=P[:, 2304:2336]), reads=[P], writes=[bgpad])
                    S.op("pe", lambda e: e.transpose(bgps[:], bgpad[:], ident[:]), reads=[bgpad, ident], writes=[bgps])
                    S.op("act", lambda e: e.activation(out=bgT[:], in_=bgps[:], func=AF.Copy), reads=[bgps], writes=[bgT])
                    S.op("pe", lambda e: e.matmul(gps[:], lhsT=bgT[:], rhs=gwfb[:], start=True, stop=True), reads=[bgT, gwfb], writes=[gps])
                    S.op("dve", lambda e: e.tensor_tensor(out=lg[:], in0=gps[:], in1=gb_bc[:], op=ALU.add), reads=[gps, gb_bc], writes=[lg])
                    S.op("act", lambda e: e.activation(out=lg[:], in_=lg[:], func=AF.Exp, scale=-1.0), reads=[lg], writes=[lg])
                    S.op("act", lambda e: e.activation(out=lg[:], in_=lg[:], func=AF.Ln, bias=1.0), reads=[lg], writes=[lg])
                    S.op("dve", lambda e: e.tensor_scalar(out=lg[:], in0=lg[:], scalar1=-1.0 / 16, scalar2=None, op0=ALU.mult), reads=[lg], writes=[lg])
                    S.dma("sp", lambda e: e.dma_start(out=LG_d[r0:r0 + 128, :], in_=lg[:]), reads=[lg], writes=[LG_d])
        ck_d, cv_d = (cak, cav) if even else (cck, ccv)
        for c in range(4):
            S.dma("sp", lambda e: e.dma_start(out=craw[:, 0:NK * 64], in_=ck_d[i, c * 128:(c + 1) * 128, :]), reads=[ck_d], writes=[craw])
            S.dma("sp", lambda e: e.dma_start(out=xt[:, 0:NK * 64], in_=cv_d[i, c * 128:(c + 1) * 128, :]), reads=[cv_d], writes=[xt])
            emit_kv(craw, craw[:, 0:NK * 64], xt, xt[:, 0:NK * 64], T + c * 128)
        S.phase_end()

    def attn_phase(l):
        even = (l % 2 == 0); i = l // 2
        dv = 64 if even else 128
        ngrp = 2 if even else 4
        S.phase_begin()
        QTz = S.sb("at_QTz", [128, 2, 2, 128], BF16); S.op("dve", lambda e: e.memset(QTz[:], 0.0), writes=[QTz])
        qt = S.sb("at_qt", [128, 256], F32)
        qps = S.ps("at_qps", [128, 2, 128])
        stp = [S.ps(f"at_stp{k}", [128, 4, 128]) for k in range(2)]
        accp = [S.ps(f"at_acc{k}", [128, 512]) for k in range(4)]
        PT = [S.sb(f"at_PT{k}", [128, 4, 128], BF16) for k in range(2)]
        LM = (NTS + 4) * 128
        nks = 1 if even else 2
        KTs = S.sb("at_KTs", [128, nks, LM], BF16); Vxs = S.sb("at_Vxs", [128, NTS + 4, nks, dv + 1], BF16)
        trib = S.sb("at_trib", [128, 2, 128], BF16)
        S.op("dve", lambda e: e.tensor_copy(out=trib[:], in_=tri[:]), reads=[tri], writes=[trib])
        og = S.sb("at_og", [128, 256], F32)
        rz = S.sb("at_rz", [128, 4], F32); o1 = S.sb("at_o1", [128, 128], F32); ssq = S.sb("at_ssq", [128, 1], F32); junk = S.sb("at_junk", [128, 128], F32)
        if even:
            esink = S.sb("at_esink", [128, 8], F32)
            S.dma("sp", lambda e: e.dma_start(out=esink[:], in_=a_sink[i, :].partition_broadcast(128)), writes=[esink])
            S.op("act", lambda e: e.activation(out=esink[:], in_=esink[:], func=AF.Exp), reads=[esink], writes=[esink])
        else:
            lam_init = 0.8 - 0.6 * math.exp(-0.3 * l)
            lq = [S.sb(f"at_lq{k}", [128, 64], F32) for k in range(4)]
            for k in range(4):
                S.dma("sp", lambda e: e.dma_start(out=lq[k][:], in_=lqk[k][i, :].partition_broadcast(128)), writes=[lq[k]])
            ld = S.sb("at_ld", [128, 2], F32); nlam = S.sb("at_nlam", [128, 1], F32)
            for k in range(2):
                S.op("dve", lambda e: e.scalar_tensor_tensor(out=junk[:, 0:64], in0=lq[2 * k][:], scalar=1.0, op0=ALU.mult, in1=lq[2 * k + 1][:], op1=ALU.mult, accum_out=ld[:, k:k + 1]),
                     reads=[lq[2 * k], lq[2 * k + 1]], writes=[junk, ld])
            S.op("act", lambda e: e.activation(out=ld[:], in_=ld[:], func=AF.Exp), reads=[ld], writes=[ld])
            S.op("dve", lambda e: e.scalar_tensor_tensor(out=nlam[:], in0=ld[:, 1:2], scalar=-lam_init, op0=ALU.add, in1=ld[:, 0:1], op1=ALU.subtract), reads=[ld], writes=[nlam])
            con = S.sb("at_con", [128, 128], F32)
            S.dma("sp", lambda e: e.dma_start(out=con[:], in_=c_on[i, :].partition_broadcast(128)), writes=[con])
            S.op("dve", lambda e: e.tensor_scalar(out=con[:], in0=con[:], scalar1=1.0 - lam_init, scalar2=None, op0=ALU.mult), reads=[con], writes=[con])
        vx_d = VXe_d if even else VXo_d
        it = 0
        for (t0, nt, s, is_s, bl) in seqs:
            nch = nt + (4 if is_s else 0)
            for grp in range(ngrp):
                ks0 = grp if even else 2 * grp
                S.dma("sp", lambda e: e.dma_start(out=KTs[:, :, 0:nt * 128], in_=KT_d[:, ks0:ks0 + nks, t0 * 128:(t0 + nt) * 128]), reads=[KT_d], writes=[KTs])
                for c in range(nt):
                    S.dma("sp", lambda e: e.dma_start(out=Vxs[:, c, :, :], in_=vx_d[(t0 + c) * 128:(t0 + c + 1) * 128, ks0:ks0 + nks, :]), reads=[vx_d], writes=[Vxs])
                if is_s:
                    S.dma("sp", lambda e: e.dma_start(out=KTs[:, :, nt * 128:(nt + 4) * 128], in_=KT_d[:, ks0:ks0 + nks, T:T + 512]), reads=[KT_d], writes=[KTs])
                    for c in range(4):
                        S.dma("sp", lambda e: e.dma_start(out=Vxs[:, nt + c, :, :], in_=vx_d[T + c * 128:T + (c + 1) * 128, ks0:ks0 + nks, :]), reads=[vx_d], writes=[Vxs])
                for n in range(nt):
                    r0 = (t0 + n) * 128
                    S.dma("sp", lambda e: e.dma_start(out=qt[:], in_=PQ_d[r0:r0 + 128, grp * 256:(grp + 1) * 256]), reads=[PQ_d], writes=[qt])
                    for pr in range(2):
                        S.op("pe", lambda e: e.transpose(qps[:, pr, :], qt[:, pr * 128:(pr + 1) * 128], ident[:]), reads=[qt, ident], writes=[qps])
                    S.op("act", lambda e: e.activation(out=QTz[0:64, :, 0, :], in_=qps[0:64, :, :], func=AF.Copy), reads=[qps], writes=[QTz])
                    S.op("act", lambda e: e.activation(out=QTz[64:128, :, 1, :], in_=qps[64:128, :, :], func=AF.Copy), reads=[qps], writes=[QTz])
                    if even and is_s:
                        chunks = ([(n - 1, 1)] if n > 0 else []) + [(n, None)] + ([(n + 1, 0)] if n < nt - 1 else []) + [(nt + c, None) for c in range(4)]
                    else:
                        chunks = [(c, None) for c in range(nch)]
                    for ci, (c, mk) in enumerate(chunks):
                        st = stp[it % 2]; pt = PT[it % 2]; it += 1
                        if even:
                            S.op("pe", lambda e: e.matmul(st[:].rearrange("p a b -> p (a b)"), lhsT=KTs[:, 0, c * 128:(c + 1) * 128], rhs=QTz[:].rearrange("p a b c -> p (a b c)"), start=True, stop=True), reads=[KTs, QTz], writes=[st])
                        else:
                            for pr in range(2):
                                S.op("pe", lambda e: e.matmul(st[:, 2 * pr:2 * pr + 2, :].rearrange("p a b -> p (a b)"), lhsT=KTs[:, pr, c * 128:(c + 1) * 128], rhs=QTz[:, pr, :, :].rearrange("p b c -> p (b c)"), start=True, stop=True), reads=[KTs, QTz], writes=[st])
                        S.op("act", lambda e: e.activation(out=pt[:], in_=st[:], func=AF.Exp, scale=0.125), reads=[st], writes=[pt])
                        if mk is not None:
                            S.op("dve", lambda e: e.tensor_tensor(out=pt[:], in0=pt[:], in1=trib[:, mk, :].unsqueeze(1).broadcast_to([128, 4, 128]), op=ALU.mult), reads=[trib], writes=[pt])
                        for g in range(4):
                            ksl = 0 if even else g // 2
                            S.op("pe", lambda e: e.matmul(accp[g][:, 0:dv + 1], lhsT=pt[:, g, :], rhs=Vxs[:, c, ksl, :], start=(ci == 0), stop=(ci == len(chunks) - 1)), reads=[pt, Vxs], writes=[accp[g]])
                    if even:
                        for g in range(4):
                            S.op("dve", lambda e: e.tensor_tensor(out=rz[:, g:g + 1], in0=accp[g][:, 64:65], in1=esink[:, grp * 4 + g:grp * 4 + g + 1], op=ALU.add), reads=[accp[g], esink], writes=[rz])
                        S.op("dve", lambda e: e.reciprocal(out=rz[:], in_=rz[:]), reads=[rz], writes=[rz])
                        for g in range(4):
                            S.op("dve", lambda e: e.tensor_scalar(out=og[:, g * 64:(g + 1) * 64], in0=accp[g][:, 0:64], scalar1=rz[:, g:g + 1], scalar2=None, op0=ALU.mult), reads=[accp[g], rz], writes=[og])
                    else:
                        for g in range(4):
                            S.op("dve", lambda e: e.reciprocal(out=rz[:, g:g + 1], in_=accp[g][:, 128:129]), reads=[accp[g]], writes=[rz])
                        for hh in range(2):
                            g1, g2 = 2 * hh, 2 * hh + 1
                            S.op("dve", lambda e: e.tensor_scalar(out=rz[:, g2:g2 + 1], in0=rz[:, g2:g2 + 1], scalar1=nlam[:, 0:1], scalar2=None, op0=ALU.mult), reads=[rz, nlam], writes=[rz])
                            S.op("dve", lambda e: e.tensor_scalar(out=o1[:], in0=accp[g1][:, 0:128], scalar1=rz[:, g1:g1 + 1], scalar2=None, op0=ALU.mult), reads=[accp[g1], rz], writes=[o1])
                            S.op("dve", lambda e: e.scalar_tensor_tensor(out=o1[:], in0=accp[g2][:, 0:128], scalar=rz[:, g2:g2 + 1], op0=ALU.mult, in1=o1[:], op1=ALU.add), reads=[accp[g2], rz], writes=[o1])
                            S.op("act", lambda e: e.activation(out=junk[:], in_=o1[:], func=AF.Square, accum_out=ssq[:]), reads=[o1], writes=[junk, ssq])
                            S.op("dve", lambda e: e.tensor_scalar(out=ssq[:], in0=ssq[:], scalar1=1.0 / 128, scalar2=EPS, op0=ALU.mult, op1=ALU.add), reads=[ssq], writes=[ssq])
                            S.op("act", lambda e: e.activation(out=ssq[:], in_=ssq[:], func=AF.Sqrt), reads=[ssq], writes=[ssq])
                            S.op("dve", lambda e: e.reciprocal(out=ssq[:], in_=ssq[:]), reads=[ssq], writes=[ssq])
                            S.op("dve", lambda e: e.scalar_tensor_tensor(out=og[:, hh * 128:(hh + 1) * 128], in0=o1[:], scalar=ssq[:, 0:1], op0=ALU.mult, in1=con[:], op1=ALU.mult), reads=[o1, ssq, con], writes=[og])
                    S.dma("sp", lambda e: e.dma_start(out=OA_d[r0:r0 + 128, grp * 256:(grp + 1) * 256], in_=og[:]), reads=[og], writes=[OA_d])
        S.phase_end()

    def gla_phase(l):
        i = l // 2
        S.phase_begin()
        qv = S.sb("gl_qv", [128, 1536], F32); lgt = S.sb("gl_lg", [128, 512], F32)
        bps = S.ps("gl_bps", [128, 2, 256])
        bsb = S.sb("gl_b", [128, 256], F32); eb = S.sb("gl_eb", [128, 256], F32); enb = S.sb("gl_enb", [128, 256], F32); ebl = S.sb("gl_ebl", [128, 256], F32)
        qdp = S.sb("gl_qdp", [128, 4, 128], F32); kdp = S.sb("gl_kdp", [128, 4, 128], F32); ktp = S.sb("gl_ktp", [128, 4, 128], F32); lgp = S.sb("gl_lgp", [128, 4, 128], F32)
        for b_ in (qdp, kdp, ktp, lgp):
            S.op("dve", lambda e: e.memset(b_[:], 0.0), writes=[b_])
        tp = [S.ps(f"gl_tp{k}", [128, 4, 128]) for k in range(2)]
        qdT = S.sb("gl_qdT", [128, 4, 128], F32); kdT = S.sb("gl_kdT", [128, 4, 128], F32)
        atp = S.ps("gl_atp", [128, 4, 128]); AT = S.sb("gl_AT", [128, 4, 128], F32)
        ops_ = S.ps("gl_ops", [128, 4, 128]); kvp = S.ps("gl_kvp", [128, 4, 128]); dps = S.ps("gl_dps", [128, 4, 128])
        dec = S.sb("gl_dec", [128, 4], F32)
        St = S.sb("gl_St", [128, 4, 128], F32)
        ob = S.sb("gl_ob", [128, 512], F32)
        for (t0, nt, s, is_s, bl) in seqs:
            for d in range(2):
                S.op("dve", lambda e: e.memset(St[:], 0.0), writes=[St])
                if is_s:
                    sd = sbf if d == 0 else sbb
                    S.dma("sp", lambda e: e.dma_start(out=St[0:64, :, :], in_=sd[i, :, :, :].rearrange("h k v -> k h v")), reads=[sd], writes=[St])
                order = range(nt) if d == 0 else range(nt - 1, -1, -1)
                for n in order:
                    r0 = (t0 + n) * 128
                    S.dma("sp", lambda e: e.dma_start(out=qv[:], in_=GL_d[r0:r0 + 128, :]), reads=[GL_d], writes=[qv])
                    S.dma("sp", lambda e: e.dma_start(out=lgt[:], in_=LG_d[r0:r0 + 128, :]), reads=[LG_d], writes=[lgt])
                    lgd = lgt[:, d * 256:(d + 1) * 256]
                    if GSTOP <= 0: continue
                    if GSUB >= 1:
                        S.op("pe", lambda e: e.matmul(bps[:, 0, :], lhsT=tri[:, d, :], rhs=lgd, start=True, stop=True), reads=[tri, lgt], writes=[bps])
                    if GSUB >= 2:
                        S.op("pe", lambda e: e.matmul(bps[:, 1, :], lhsT=ones_f[:], rhs=lgd, start=True, stop=True), reads=[ones_f, lgt], writes=[bps])
                    if GSUB >= 1:
                        S.op("act", lambda e: e.activation(out=bsb[:], in_=bps[:, 0, :], func=AF.Copy), reads=[bps], writes=[bsb])
                    if GSUB >= 3:
                        S.op("act", lambda e: e.activation(out=eb[:], in_=bps[:, 0, :], func=AF.Exp), reads=[bps], writes=[eb])
                    if GSUB >= 4:
                        S.op("act", lambda e: e.activation(out=enb[:], in_=bps[:, 0, :], func=AF.Exp, scale=-1.0), reads=[bps], writes=[enb])
                    if GSUB >= 5:
                        S.op("act", lambda e: e.activation(out=ebl[:], in_=bps[:, 1, :], func=AF.Exp), reads=[bps], writes=[ebl])
                    if GSUB >= 6:
                        S.op("dve", lambda e: e.tensor_tensor(out=ebl[:], in0=ebl[:], in1=enb[:], op=ALU.mult), reads=[enb], writes=[ebl])
                    if GSTOP <= 1: continue
                    q3 = qv[:, 0:256].rearrange("p (h d) -> p h d", d=64); k3 = qv[:, 256:512].rearrange("p (h d) -> p h d", d=64)
                    e3 = lambda b_: b_[:].rearrange("p (h d) -> p h d", d=64)
                    S.op("dve", lambda e: e.scalar_tensor_tensor(out=qdp[:, :, 0:64], in0=q3, scalar=0.125, op0=ALU.mult, in1=e3(eb), op1=ALU.mult), reads=[qv, eb], writes=[qdp])
                    S.op("dve", lambda e: e.tensor_tensor(out=kdp[:, :, 0:64], in0=k3, in1=e3(enb), op=ALU.mult), reads=[qv, enb], writes=[kdp])
                    S.op("pool", lambda e: e.tensor_tensor(out=ktp[:, :, 0:64], in0=k3, in1=e3(ebl), op=ALU.mult), reads=[qv, ebl], writes=[ktp])
                    S.op("pool", lambda e: e.tensor_copy(out=lgp[:, :, 0:64], in_=lgd.rearrange("p (h d) -> p h d", d=64)), reads=[lgt], writes=[lgp])
                    if GSTOP <= 2: continue
                    for h in range(4):
                        S.op("pe", lambda e: e.transpose(tp[0][:, h, :], qdp[:, h, :], ident[:]), reads=[qdp, ident], writes=[tp[0]])
                    for h in range(4):
                        S.op("pe", lambda e: e.transpose(tp[1][:, h, :], kdp[:, h, :], ident[:]), reads=[kdp, ident], writes=[tp[1]])
                    S.op("act", lambda e: e.activation(out=qdT[:], in_=tp[0][:], func=AF.Copy), reads=[tp[0]], writes=[qdT])
                    S.op("act", lambda e: e.activation(out=kdT[:], in_=tp[1][:], func=AF.Copy), reads=[tp[1]], writes=[kdT])
                    if GSTOP <= 3: continue
                    for h in range(4):
                        S.op("pe", lambda e: e.matmul(atp[:, h, :], lhsT=kdT[:, h, :], rhs=qdT[:, h, :], start=True, stop=True), reads=[kdT, qdT], writes=[atp])
                    S.op("dve", lambda e: e.tensor_tensor(out=AT[:], in0=atp[:], in1=tri[:, d, :].unsqueeze(1).broadcast_to([128, 4, 128]), op=ALU.mult), reads=[atp, tri], writes=[AT])
                    if GSTOP <= 4: continue
                    v3 = qv[:, 512:1024].rearrange("p (h d) -> p h d", d=128)
                    for h in range(4):
                        S.op("pe", lambda e: e.matmul(ops_[:, h, :], lhsT=AT[:, h, :], rhs=v3[:, h, :], start=True, stop=False), reads=[AT, qv], writes=[ops_])
                        S.op("pe", lambda e: e.matmul(ops_[:, h, :], lhsT=qdT[:, h, :], rhs=St[:, h, :], start=False, stop=True), reads=[qdT, St], writes=[ops_])
                    S.op("act", lambda e: e.activation(out=ob[:], in_=ops_[:].rearrange("p h d -> p (h d)"), func=AF.Copy), reads=[ops_], writes=[ob])
                    S.dma("sp", lambda e: e.dma_start(out=OB_d[d][r0:r0 + 128, :], in_=ob[:]), reads=[ob], writes=[OB_d[d]])
                    if GSTOP <= 5: continue
                    for h in range(4):
                        S.op("pe", lambda e: e.matmul(kvp[:, h, :], lhsT=ktp[:, h, :], rhs=v3[:, h, :], start=True, stop=True), reads=[ktp, qv], writes=[kvp])
                        S.op("pe", lambda e: e.matmul(dps[:, h, :], lhsT=lgp[:, h, :], rhs=ones_f[:], start=True, stop=True), reads=[lgp, ones_f], writes=[dps])
                    S.op("act", lambda e: e.activation(out=dec[:], in_=dps[:, :, 0], func=AF.Exp), reads=[dps], writes=[dec])
                    for h in range(4):
                        S.op("dve", lambda e: e.scalar_tensor_tensor(out=St[:, h, :], in0=St[:, h, :], scalar=dec[:, h:h + 1], op0=ALU.mult, in1=kvp[:, h, :], op1=ALU.add), reads=[dec, kvp], writes=[St])
                if not is_s:
                    so = nsf if d == 0 else nsb
                    S.dma("sp", lambda e: e.dma_start(out=so[bl, i, :, :, :].rearrange("h k v -> k h v"), in_=St[0:64, :, :]), reads=[St], writes=[so])
        S.phase_end()

    def out_phase(l, st_in, st_out):
        even = (l % 2 == 0); i = l // 2
        S.phase_begin()
        W = S.sb("op_W", [128, 8, 1024], BF16)
        w_d = e_w_out if even else o_w_out
        for dc in range(8):
            S.dma("pool", lambda e: e.dma_start(out=W[:, dc, :], in_=w_d[i, dc * 128:(dc + 1) * 128, :]), reads=[w_d], writes=[W])
        oc = S.sb("op_oc", [128, 1024], F32); xt = S.sb("op_x", [128, 1024], F32); xo = S.sb("op_xo", [128, 1024], F32)
        tps = [S.ps(f"op_tps{k}", [128, 4, 128]) for k in range(2)]
        oT = S.sb("op_oT", [128, 8, 128], BF16)
        pps = [S.ps(f"op_pps{k}", [128, 512]) for k in range(2)]
        if even:
            obb = S.sb("op_obb", [128, 512], F32); br = S.sb("op_br", [128, 512], F32)
            sq = S.sb("op_sq", [128, 512], F32); ss = S.sb("op_ss", [128, 4], F32)
            bon = S.sb("op_bon", [128, 128], F32)
            S.dma("sp", lambda e: e.dma_start(out=bon[:], in_=b_on[i, :].partition_broadcast(128)), writes=[bon])
        xi, xod = res(st_in), res(st_out)
        for t in range(NT):
            s = tile_set(t); r0 = t * 128
            S.dma("sp", lambda e: e.dma_start(out=xt[:], in_=xi[r0:r0 + 128, :]), reads=[xi], writes=[xt])
            if even:
                S.dma("sp", lambda e: e.dma_start(out=oc[:, 0:512], in_=OA_d[r0:r0 + 128, 0:512]), reads=[OA_d], writes=[oc])
                S.dma("sp", lambda e: e.dma_start(out=oc[:, 512:1024], in_=OB_d[0][r0:r0 + 128, :]), reads=[OB_d[0]], writes=[oc])
                S.dma("sp", lambda e: e.dma_start(out=obb[:], in_=OB_d[1][r0:r0 + 128, :]), reads=[OB_d[1]], writes=[obb])
                S.dma("sp", lambda e: e.dma_start(out=br[:], in_=GL_d[r0:r0 + 128, 1024:1536]), reads=[GL_d], writes=[br])
                o2 = oc[:, 512:1024]
                S.op("dve", lambda e: e.tensor_tensor(out=o2, in0=o2, in1=obb[:], op=ALU.add), reads=[obb], writes=[oc])
                S.op("dve", lambda e: e.tensor_tensor(out=sq[:], in0=o2, in1=o2, op=ALU.mult), reads=[oc], writes=[sq])
                S.op("dve", lambda e: e.tensor_reduce(out=ss[:], in_=sq[:].rearrange("p (h d) -> p h d", d=128), axis=AX.X, op=ALU.add), reads=[sq], writes=[ss])
                S.op("dve", lambda e: e.tensor_scalar(out=ss[:], in0=ss[:], scalar1=1.0 / 128, scalar2=EPS, op0=ALU.mult, op1=ALU.add), reads=[ss], writes=[ss])
                S.op("act", lambda e: e.activation(out=ss[:], in_=ss[:], func=AF.Sqrt), reads=[ss], writes=[ss])
                S.op("dve", lambda e: e.reciprocal(out=ss[:], in_=ss[:]), reads=[ss], writes=[ss])
                o3 = o2.rearrange("p (h d) -> p h d", d=128)
                S.op("dve", lambda e: e.tensor_tensor(out=o3, in0=o3, in1=ss[:].unsqueeze(2).broadcast_to([128, 4, 128]), op=ALU.mult), reads=[ss], writes=[oc])
                S.op("dve", lambda e: e.tensor_tensor(out=o3, in0=o3, in1=bon[:].unsqueeze(1).broadcast_to([128, 4, 128]), op=ALU.mult), reads=[bon], writes=[oc])
                S.op("act", lambda e: e.activation(out=br[:], in_=br[:], func=AF.Silu), reads=[br], writes=[br])
                S.op("dve", lambda e: e.tensor_tensor(out=o2, in0=o2, in1=br[:], op=ALU.mult), reads=[br], writes=[oc])
            else:
                S.dma("sp", lambda e: e.dma_start(out=oc[:], in_=OA_d[r0:r0 + 128, :]), reads=[OA_d], writes=[oc])
            for dc in range(8):
                p = tps[dc // 4]
                S.op("pe", lambda e: e.transpose(p[:, dc % 4, :], oc[:, dc * 128:(dc + 1) * 128], ident[:]), reads=[oc, ident], writes=[p])
            for k in range(2):
                S.op("act", lambda e: e.activation(out=oT[:, 4 * k:4 * k + 4, :], in_=tps[k][:], func=AF.Copy), reads=[tps[k]], writes=[oT])
            for c in range(2):
                p = pps[c]
                for dc in range(8):
                    S.op("pe", lambda e: e.matmul(p[:], lhsT=oT[:, dc, :], rhs=W[:, dc, c * 512:(c + 1) * 512], start=(dc == 0), stop=(dc == 7)), reads=[oT, W], writes=[p])
                S.op("dve", lambda e: e.tensor_tensor(out=xo[:, c * 512:(c + 1) * 512], in0=p[:], in1=mod[s][:, 2 * 1024 + c * 512:2 * 1024 + (c + 1) * 512], op=ALU.mult), reads=[p, mod[s]], writes=[xo])
            S.op("pool", lambda e: e.tensor_tensor(out=xo[:], in0=xo[:], in1=xt[:], op=ALU.add), reads=[xt], writes=[xo])
            S.dma("sp", lambda e: e.dma_start(out=xod[r0:r0 + 128, :], in_=xo[:]), reads=[xo], writes=[xod])
        S.phase_end()

    def mixer_layer(l, st_in, st_out):
        proj_phase(l, st_in)
        if PHASES >= 2: attn_phase(l)
        if l % 2 == 0 and PHASES >= 3:
            gla_phase(l)
        if PHASES >= 4: out_phase(l, st_in, st_out)


    for l in range(n_layers):
        modulation(l)
        if do_mixer:
            mixer_layer(l, 2 * l, 2 * l + 1)
        else:
            pass
        if do_peer:
            peer_layer(l, 2 * l + 1, 2 * l + 2)
    S.finish()
    return nc, S

_HOST = {}
def rope_table(nts):
    Tn = nts * 128
    t = np.arange(Tn)
    row = (t // 64).astype(np.float32); col = (t % 64).astype(np.float32)
    freqs = (np.float32(10000.0) ** (-np.arange(16, dtype=np.float32) / np.float32(16))).astype(np.float32)
    ang = np.stack([row[:, None] * freqs, col[:, None] * freqs], axis=1).astype(np.float32)
    return np.concatenate([np.cos(ang).reshape(Tn, 32), np.sin(ang).reshape(Tn, 32)], axis=1).astype(np.float32)

def host_inputs(inp, core, xin=None):
    f = lambda a: np.ascontiguousarray(np.asarray(a, dtype=np.float32))
    b = core % 4
    if xin is None:
        xin = np.concatenate([f(inp["x_prompt"])[4 * core:4 * core + 4].reshape(1024, 1024), f(inp["x_sample"])[b]], axis=0)
    cvec = np.stack([f(inp["c_ctx"]), f(inp["c"])[b]], axis=0)
    cT = np.ascontiguousarray(cvec.reshape(2, 8, 128).transpose(2, 1, 0))
    sk = f(inp["p_sub_keys"])
    skz = np.zeros((4, 2, 64, 8, 2, 128), np.float32)
    for pp in range(2):
        skz[:, pp, :, :, pp, :] = sk[:, :, pp].transpose(0, 3, 1, 2)
    skz = skz.reshape(4, 128, 8, 2, 128)
    jj = np.arange(128)[:, None]; ii = np.arange(128)[None, :]
    tri = np.stack([(jj <= ii), (jj >= ii)]).astype(np.float32)
    m = dict(xin=xin, cT=cT, ada_w=f(inp["ada_w"]), ada_b=f(inp["ada_b"]), norm_mix_g=f(inp["norm_mix_g"]), norm_ffn_g=f(inp["norm_ffn_g"]),
             p_w_q=f(inp["p_w_q"]), skz=skz, ident=np.eye(128, dtype=np.float32), tri=tri, rope=rope_table(NTS),
             e_w_in=f(inp["e_w_in"]), e_w_out=f(inp["e_w_out"]), o_w_in=f(inp["o_w_in"]), o_w_out=f(inp["o_w_out"]),
             a_q_norm=f(inp["a_q_norm"]), a_k_norm=f(inp["a_k_norm"]), a_sink=f(inp["a_sink"]).reshape(2, 8),
             b_gate_w_f=f(inp["b_gate_w_f"]), b_gate_b_f=f(inp["b_gate_b_f"]), b_gate_w_b=f(inp["b_gate_w_b"]), b_gate_b_b=f(inp["b_gate_b_b"]),
             b_out_norm=f(inp["b_out_norm"]), c_q_norm=f(inp["c_q_norm"]), c_k_norm=f(inp["c_k_norm"]), c_out_norm=f(inp["c_out_norm"]),
             c_lambda_q1=f(inp["c_lambda_q1"]), c_lambda_k1=f(inp["c_lambda_k1"]), c_lambda_q2=f(inp["c_lambda_q2"]), c_lambda_k2=f(inp["c_lambda_k2"]),
             cak=f(inp["cache_a_k"])[b].reshape(2, 512, 128), cav=f(inp["cache_a_v"])[b].reshape(2, 512, 128),
             sbf=f(inp["state_b_fwd"])[b], sbb=f(inp["state_b_bwd"])[b],
             cck=f(inp["cache_c_k"])[b].reshape(2, 512, 1024), ccv=f(inp["cache_c_v"])[b].reshape(2, 512, 1024))
    if "uT" not in _HOST:
        _HOST["uT"] = [np.ascontiguousarray(f(inp["p_u"])[l].reshape(64, 256, 8, 128).transpose(0, 3, 2, 1)).reshape(64, 128, 2048) for l in range(4)]
        _HOST["v"] = [np.ascontiguousarray(f(inp["p_v"])[l].reshape(64, 2, 128, 1024).transpose(0, 2, 1, 3)).reshape(64, 128, 2048) for l in range(4)]
    for l in range(4):
        m[f"p_uT{l}"] = _HOST["uT"][l]; m[f"p_v{l}"] = _HOST["v"][l]
    return m

_CACHE = {}
def kernel(**inp):
    _HOST.clear()
    if "nc" not in _CACHE:
        _CACHE["nc"] = build_program()[0]
    nc = _CACHE["nc"]
    in_maps = [host_inputs(inp, c) for c in range(8)]
    res = run_bass_kernel_spmd(nc, in_maps, core_ids=list(range(8)))
    R = res.results
    cat = lambda k: np.concatenate([R[c][k] for c in range(8)], axis=0)
    y_prompt = np.concatenate([R[c]["yout"][:1024].reshape(4, 256, 1024) for c in range(8)], axis=0)
    y_sample = np.stack([R[c]["yout"][1024:] for c in range(4)], axis=0)
    return (y_prompt, y_sample, cat("nak").reshape(32, 2, 256, 2, 64), cat("nav").reshape(32, 2, 256, 2, 64),
            cat("nsf"), cat("nsb"), cat("nck").reshape(32, 2, 256, 8, 2, 64), cat("ncv").reshape(32, 2, 256, 8, 128))
```
